# Optimizing a Trainium2 kernel written in Bass

```python
import math
import jax, jax.numpy as jnp
from jax import lax
import numpy as np

D_MODEL = 1024
BATCH = 8
SEQ = 2048
DEPTH = 4

DA_HEADS = 4
DA_HEAD_DIM = 64
DA_QK_WIDTH = DA_HEADS * 2 * DA_HEAD_DIM
DA_WIDTH = DA_HEADS * 2 * DA_HEAD_DIM
MLA_HEADS = 8
MLA_Q_RANK = 256
MLA_KV_RANK = 128
MLA_NOPE = 64
MLA_ROPE = 32
MLA_V = 64
MLA_QK = MLA_NOPE + MLA_ROPE
MLA_WIDTH = MLA_HEADS * MLA_V
SSM_D_INNER = D_MODEL
SSM_HEAD_DIM = 64
SSM_HEADS = SSM_D_INNER // SSM_HEAD_DIM
SSM_GROUPS = 4
SSM_STATE = 128
SSM_CONV = 4
SSM_CHUNK = 128
SSM_CONV_DIM = SSM_D_INNER + 2 * SSM_GROUPS * SSM_STATE
FFN_DIM = 2816
FFN_CONV = 3
REL_BUCKETS = 32
REL_MAX_DIST = 128
ROPE_THETA = 10000.0
Q_BLOCK = 128
EPS = 1e-6

_IN_SIZES = (DA_QK_WIDTH, DA_QK_WIDTH, DA_WIDTH, MLA_Q_RANK, MLA_KV_RANK, MLA_ROPE,
             SSM_D_INNER, SSM_CONV_DIM, SSM_HEADS, 3 * D_MODEL)
IN_COLS = sum(_IN_SIZES)
_IN_OFFSETS = tuple(int(v) for v in np.cumsum(_IN_SIZES)[:-1])

kernel_name = 'hybrid_diffattn_mla_ssd_gated_trunk'

f32 = jnp.float32


def _rmsnorm(x, w):
    xf = x.astype(f32)
    y = xf * lax.rsqrt(jnp.mean(xf * xf, axis=-1, keepdims=True) + EPS)
    return (y * w.astype(f32)).astype(x.dtype)


def _causal_dwconv(u, w, b):
    k = w.shape[0]
    out = lax.conv_general_dilated(u, w[:, None, :].astype(u.dtype), window_strides=(1,),
                                   padding=[(k - 1, 0)], dimension_numbers=('NWC', 'WIO', 'NWC'),
                                   feature_group_count=u.shape[-1])
    return out + b.astype(u.dtype)


def _rope(x, cos, sin):
    half = x.shape[-1] // 2
    x1, x2 = x[..., :half], x[..., half:]
    return jnp.concatenate([x1 * cos - x2 * sin, x2 * cos + x1 * sin], axis=-1).astype(x.dtype)


def _t5_bucket(dist):
    max_exact = REL_BUCKETS // 2
    d = jnp.maximum(dist, 0)
    large = max_exact + (jnp.log(jnp.maximum(d, 1).astype(f32) / max_exact)
                         / math.log(REL_MAX_DIST / max_exact) * (REL_BUCKETS - max_exact)).astype(jnp.int32)
    large = jnp.minimum(large, REL_BUCKETS - 1)
    return jnp.where(d < max_exact, d, large)


def _to_blocks(a):
    b, h, s = a.shape[:3]
    return jnp.moveaxis(a.reshape(b, h, s // Q_BLOCK, Q_BLOCK, *a.shape[3:]), 2, 0)


def _from_blocks(a):
    a = jnp.moveaxis(a, 0, 2)
    return a.reshape(a.shape[0], a.shape[1], a.shape[2] * a.shape[3], *a.shape[4:])


def _causal_mask(blk, seq):
    q_idx = blk * Q_BLOCK + jnp.arange(Q_BLOCK)
    return jnp.arange(seq)[None, :] <= q_idx[:, None]


def _diff_attention(q, k, v, positions, rel_table, lam):
    bsz, _, seq = q.shape[:3]
    nb = seq // Q_BLOCK
    scale = DA_HEAD_DIM ** -0.5
    pos_blocks = jnp.moveaxis(positions.reshape(bsz, nb, Q_BLOCK), 1, 0)

    def block(args):
        qb, qpos, blk = args
        s = jnp.einsum('bhqmd,bhkmd->mbhqk', qb, k).astype(f32) * scale
        dist = qpos[:, :, None] - positions[:, None, :]
        bias = jnp.take(rel_table, _t5_bucket(dist), axis=0)
        s = s + jnp.transpose(bias, (0, 3, 1, 2)).astype(f32)[None]
        s = jnp.where(_causal_mask(blk, seq), s, -jnp.inf)
        p = jax.nn.softmax(s, axis=-1)
        w = p[0] - lam * p[1]
        return jnp.einsum('bhqk,bhkv->bhqv', w.astype(v.dtype), v)

    return _from_blocks(lax.map(block, (_to_blocks(q), pos_blocks, jnp.arange(nb))))


def _mla_attention(q, k, v):
    seq = q.shape[2]
    nb = seq // Q_BLOCK
    scale = MLA_QK ** -0.5

    def block(args):
        qb, blk = args
        s = jnp.einsum('bhqd,bhkd->bhqk', qb, k).astype(f32) * scale
        s = jnp.where(_causal_mask(blk, seq), s, -jnp.inf)
        p = jax.nn.softmax(s, axis=-1)
        return jnp.einsum('bhqk,bhkd->bhqd', p.astype(v.dtype), v)

    return _from_blocks(lax.map(block, (_to_blocks(q), jnp.arange(nb))))


def _ssd(xs, dt, a, bm, cm):
    bsz, seq = xs.shape[:2]
    nc, L = seq // SSM_CHUNK, SSM_CHUNK
    G, R = SSM_GROUPS, SSM_HEADS // SSM_GROUPS
    xdt = (xs.astype(f32) * dt[..., None]).reshape(bsz, nc, L, G, R, SSM_HEAD_DIM)
    da = (dt * a).reshape(bsz, nc, L, G, R)
    bm = bm.astype(f32).reshape(bsz, nc, L, G, SSM_STATE)
    cm = cm.astype(f32).reshape(bsz, nc, L, G, SSM_STATE)
    cs = jnp.cumsum(da, axis=2)
    causal = jnp.tril(jnp.ones((L, L), bool))[None, None, :, :, None, None]
    seg = cs[:, :, :, None] - cs[:, :, None, :]
    decay = jnp.exp(jnp.where(causal, seg, -jnp.inf))
    cb = jnp.einsum('bclgn,bcsgn->bclsg', cm, bm)
    y_diag = jnp.einsum('bclsgr,bcsgrp->bclgrp', cb[..., None] * decay, xdt)
    to_end = jnp.exp(cs[:, :, -1:] - cs)
    states = jnp.einsum('bclgn,bclgrp->bcgrpn', bm, xdt * to_end[..., None])
    chunk_decay = jnp.exp(cs[:, :, -1])

    def step(h, inp):
        st, dec = inp
        return dec[..., None, None] * h + st, h

    init = jnp.zeros((bsz, G, R, SSM_HEAD_DIM, SSM_STATE), f32)
    _, prev = lax.scan(step, init, (jnp.moveaxis(states, 1, 0), jnp.moveaxis(chunk_decay, 1, 0)))
    prev = jnp.moveaxis(prev, 0, 1)
    y_off = jnp.einsum('bclgn,bcgrpn->bclgrp', cm, prev) * jnp.exp(cs)[..., None]
    return (y_diag + y_off).reshape(bsz, seq, SSM_HEADS, SSM_HEAD_DIM)


def setup_inputs(seed: int = 0) -> dict:
    key = jax.random.key(seed)
    ks = iter(jax.random.split(key, 40))

    def nrm(shape, scale):
        return jax.random.normal(next(ks), shape, f32) * scale

    def gain(shape):
        return 1.0 + nrm(shape, 0.02)

    L = DEPTH
    x = nrm((BATCH, SEQ, D_MODEL), 1.0)
    c = nrm((BATCH, D_MODEL), 1.0)
    positions = jnp.arange(SEQ, dtype=jnp.int32)[None, :] + jax.random.randint(
        next(ks), (BATCH, 1), 0, 1024, dtype=jnp.int32)
    rel_bias = nrm((REL_BUCKETS, DA_HEADS), 0.5)
    ada_w = nrm((L, D_MODEL, 6 * D_MODEL), D_MODEL ** -0.5)
    ada_b = nrm((L, 6 * D_MODEL), 0.02)
    norm_mix = gain((L, D_MODEL))
    norm_ffn = gain((L, D_MODEL))
    w_in = nrm((L, D_MODEL, IN_COLS), D_MODEL ** -0.5)
    da_q_norm = gain((L, DA_HEAD_DIM))
    da_k_norm = gain((L, DA_HEAD_DIM))
    da_lambda = nrm((L, 4, DA_HEAD_DIM), 0.1)
    da_subln = gain((L, 2 * DA_HEAD_DIM))
    mla_q_a_norm = gain((L, MLA_Q_RANK))
    mla_kv_a_norm = gain((L, MLA_KV_RANK))
    mla_w_uq = nrm((L, MLA_Q_RANK, MLA_HEADS * MLA_QK), MLA_Q_RANK ** -0.5)
    mla_w_ukv = nrm((L, MLA_KV_RANK, MLA_HEADS * (MLA_NOPE + MLA_V)), MLA_KV_RANK ** -0.5)
    mla_q_norm = gain((L, MLA_QK))
    mla_k_norm = gain((L, MLA_QK))
    ssm_conv_w = nrm((L, SSM_CONV, SSM_CONV_DIM), SSM_CONV ** -0.5)
    ssm_conv_b = nrm((L, SSM_CONV_DIM), 0.02)
    dt0 = jnp.exp(jax.random.uniform(next(ks), (L, SSM_HEADS), f32, math.log(1e-3), math.log(1e-1)))
    ssm_dt_bias = dt0 + jnp.log(-jnp.expm1(-dt0))
    ssm_a_log = jnp.log(jax.random.uniform(next(ks), (L, SSM_HEADS), f32, 1.0, 16.0))
    ssm_d = 1.0 + nrm((L, SSM_HEADS), 0.1)
    ssm_norm = gain((L, SSM_D_INNER))
    w_branch_a = nrm((L, DA_WIDTH, D_MODEL), DA_WIDTH ** -0.5)
    w_branch_b = nrm((L, MLA_WIDTH, D_MODEL), MLA_WIDTH ** -0.5)
    w_branch_c = nrm((L, SSM_D_INNER, D_MODEL), SSM_D_INNER ** -0.5)
    w_out = nrm((L, D_MODEL, D_MODEL), D_MODEL ** -0.5)
    ffn_w_up = nrm((L, D_MODEL, 2 * FFN_DIM), D_MODEL ** -0.5)
    ffn_conv_w = nrm((L, FFN_CONV, 2 * FFN_DIM), FFN_CONV ** -0.5)
    ffn_conv_b = nrm((L, 2 * FFN_DIM), 0.02)
    ffn_w_down = nrm((L, FFN_DIM, D_MODEL), FFN_DIM ** -0.5)
    return {'x': x, 'c': c, 'positions': positions, 'rel_bias': rel_bias,
            'ada_w': ada_w, 'ada_b': ada_b, 'norm_mix': norm_mix, 'norm_ffn': norm_ffn, 'w_in': w_in,
            'da_q_norm': da_q_norm, 'da_k_norm': da_k_norm, 'da_lambda': da_lambda, 'da_subln': da_subln,
            'mla_q_a_norm': mla_q_a_norm, 'mla_kv_a_norm': mla_kv_a_norm, 'mla_w_uq': mla_w_uq,
            'mla_w_ukv': mla_w_ukv, 'mla_q_norm': mla_q_norm, 'mla_k_norm': mla_k_norm,
            'ssm_conv_w': ssm_conv_w, 'ssm_conv_b': ssm_conv_b, 'ssm_dt_bias': ssm_dt_bias,
            'ssm_a_log': ssm_a_log, 'ssm_d': ssm_d, 'ssm_norm': ssm_norm,
            'w_branch_a': w_branch_a, 'w_branch_b': w_branch_b, 'w_branch_c': w_branch_c, 'w_out': w_out,
            'ffn_w_up': ffn_w_up, 'ffn_conv_w': ffn_conv_w, 'ffn_conv_b': ffn_conv_b, 'ffn_w_down': ffn_w_down}


def reference(x, c, positions, rel_bias, ada_w, ada_b, norm_mix, norm_ffn, w_in,
              da_q_norm, da_k_norm, da_lambda, da_subln,
              mla_q_a_norm, mla_kv_a_norm, mla_w_uq, mla_w_ukv, mla_q_norm, mla_k_norm,
              ssm_conv_w, ssm_conv_b, ssm_dt_bias, ssm_a_log, ssm_d, ssm_norm,
              w_branch_a, w_branch_b, w_branch_c, w_out,
              ffn_w_up, ffn_conv_w, ffn_conv_b, ffn_w_down):
    bsz, seq, _ = x.shape
    inv_freq = ROPE_THETA ** (-jnp.arange(0, MLA_ROPE, 2, dtype=f32) / MLA_ROPE)
    ang = positions.astype(f32)[..., None] * inv_freq
    cos, sin = jnp.cos(ang), jnp.sin(ang)
    c_act = jax.nn.silu(c)

    for l in range(DEPTH):
        mod = c_act @ ada_w[l] + ada_b[l]
        sh_m, sc_m, g_m, sh_f, sc_f, g_f = jnp.split(mod[:, None, :], 6, axis=-1)

        h = _rmsnorm(x, norm_mix[l]) * (1 + sc_m) + sh_m
        proj = h @ w_in[l]
        qa, ka, va, cq, ckv, kr, z, xbc, dt_raw, gates = jnp.split(proj, _IN_OFFSETS, axis=-1)

        qa = _rmsnorm(qa.reshape(bsz, seq, DA_HEADS, 2, DA_HEAD_DIM), da_q_norm[l]).transpose(0, 2, 1, 3, 4)
        ka = _rmsnorm(ka.reshape(bsz, seq, DA_HEADS, 2, DA_HEAD_DIM), da_k_norm[l]).transpose(0, 2, 1, 3, 4)
        va = va.reshape(bsz, seq, DA_HEADS, 2 * DA_HEAD_DIM).transpose(0, 2, 1, 3)
        lam_init = 0.8 - 0.6 * math.exp(-0.3 * l)
        lp = da_lambda[l].astype(f32)
        lam = jnp.exp(jnp.sum(lp[0] * lp[1])) - jnp.exp(jnp.sum(lp[2] * lp[3])) + lam_init
        oa = _diff_attention(qa, ka, va, positions, rel_bias, lam)
        oa = _rmsnorm(oa, da_subln[l]) * (1.0 - lam_init)
        ya = oa.transpose(0, 2, 1, 3).reshape(bsz, seq, DA_WIDTH)

        cq = _rmsnorm(cq, mla_q_a_norm[l])
        qb = (cq @ mla_w_uq[l]).reshape(bsz, seq, MLA_HEADS, MLA_QK)
        ckv = _rmsnorm(ckv, mla_kv_a_norm[l])
        kv = (ckv @ mla_w_ukv[l]).reshape(bsz, seq, MLA_HEADS, MLA_NOPE + MLA_V)
        k_nope, vb = kv[..., :MLA_NOPE], kv[..., MLA_NOPE:]
        kb = jnp.concatenate([k_nope, jnp.broadcast_to(kr[:, :, None, :], (bsz, seq, MLA_HEADS, MLA_ROPE))], axis=-1)
        qb = _rmsnorm(qb, mla_q_norm[l])
        kb = _rmsnorm(kb, mla_k_norm[l])
        cos_h, sin_h = cos[:, :, None, :], sin[:, :, None, :]
        qb = jnp.concatenate([qb[..., :MLA_NOPE], _rope(qb[..., MLA_NOPE:], cos_h, sin_h)], axis=-1)
        kb = jnp.concatenate([kb[..., :MLA_NOPE], _rope(kb[..., MLA_NOPE:], cos_h, sin_h)], axis=-1)
        ob = _mla_attention(qb.transpose(0, 2, 1, 3), kb.transpose(0, 2, 1, 3), vb.transpose(0, 2, 1, 3))
        yb = ob.transpose(0, 2, 1, 3).reshape(bsz, seq, MLA_WIDTH)

        xbc = jax.nn.silu(_causal_dwconv(xbc, ssm_conv_w[l], ssm_conv_b[l]))
        xs, bm, cm = jnp.split(xbc, [SSM_D_INNER, SSM_D_INNER + SSM_GROUPS * SSM_STATE], axis=-1)
        xs = xs.reshape(bsz, seq, SSM_HEADS, SSM_HEAD_DIM)
        bm = bm.reshape(bsz, seq, SSM_GROUPS, SSM_STATE)
        cm = cm.reshape(bsz, seq, SSM_GROUPS, SSM_STATE)
        dt = jax.nn.softplus(dt_raw.astype(f32) + ssm_dt_bias[l].astype(f32))
        a = -jnp.exp(ssm_a_log[l].astype(f32))
        ys = _ssd(xs, dt, a, bm, cm) + ssm_d[l].astype(f32)[:, None] * xs.astype(f32)
        ys = ys.reshape(bsz, seq, SSM_D_INNER) * jax.nn.silu(z.astype(f32))
        ys = _rmsnorm(ys.reshape(bsz, seq, SSM_GROUPS, -1), ssm_norm[l].reshape(SSM_GROUPS, -1))
        ys = ys.reshape(bsz, seq, SSM_D_INNER).astype(x.dtype)

        ga, gb, gc = jnp.split(jax.nn.sigmoid(gates), 3, axis=-1)
        merged = ga * (ya @ w_branch_a[l]) + gb * (yb @ w_branch_b[l]) + gc * (ys @ w_branch_c[l])
        x = x + g_m * (merged @ w_out[l])

        h = _rmsnorm(x, norm_ffn[l]) * (1 + sc_f) + sh_f
        u = _causal_dwconv(h @ ffn_w_up[l], ffn_conv_w[l], ffn_conv_b[l])
        ua, uv = jnp.split(u, 2, axis=-1)
        x = x + g_f * ((jax.nn.silu(ua) * uv) @ ffn_w_down[l])
    return x
```

```python
import math
from contextlib import ExitStack
import numpy as np
import concourse.bass as bass
import concourse.mybir as mybir
from concourse.bass_utils import run_bass_kernel_spmd

F32 = mybir.dt.float32
BF16 = mybir.dt.bfloat16
I32 = mybir.dt.int32
AF = mybir.ActivationFunctionType
ALU = mybir.AluOpType

D = 1024
S = 2048
NL = 4
NTC = 4
NTT = 16
FFN = 2816
NF = 22
IN_SIZES = (512, 512, 512, 256, 128, 32, 1024, 2048, 16, 3072)
IN_OFF = [0] + [int(v) for v in np.cumsum(IN_SIZES)]
O_QA, O_KA, O_VA, O_CQ, O_CKV, O_KR, O_Z, O_XBC, O_DT, O_G = IN_OFF[:10]
EPS = 1e-6
NEG = -30000.0


class _Op:
    __slots__ = ("i", "eng", "fn", "deps", "raw", "nss", "skip", "signal", "sidx", "dma", "dsem", "dval", "dprev")

    def __init__(self, i, eng, fn, dma):
        self.i, self.eng, self.fn, self.dma = i, eng, fn, dma
        self.deps = set()
        self.raw = set()
        self.nss = False
        self.skip = set()
        self.signal = False
        self.sidx = -1
        self.dsem = self.dval = -1
        self.dprev = None


class Prog:
    ENGS = ("pe", "act", "dve", "pool", "sp")
    EPOCH = 512
    NSEM = 8
    NDMA = 32

    def __init__(self):
        self.ops = []
        self.st = {}
        self.last = {}
        self._dmas = []
        self.muted = False

    def _node(self, name):
        n = self.st.get(name)
        if n is None:
            n = self.st[name] = {"w": [None, []], "p": {}}
        return n

    @staticmethod
    def _addreader(lst, op):
        if not op.dma:
            for k, o in enumerate(lst):
                if (not o.dma) and o.eng == op.eng:
                    lst[k] = op
                    return
        lst.append(op)

    def _collect(self, reads, writes):
        deps = set()
        for key in reads:
            n = self._node(key[0])
            if n["w"][0] is not None:
                deps.add(n["w"][0])
            if len(key) == 1:
                for p in n["p"].values():
                    if p[0] is not None:
                        deps.add(p[0])
            else:
                p = n["p"].get(key[1])
                if p is not None and p[0] is not None:
                    deps.add(p[0])
        for key in writes:
            n = self._node(key[0])
            if n["w"][0] is not None:
                deps.add(n["w"][0])
            deps.update(n["w"][1])
            if len(key) == 1:
                for p in n["p"].values():
                    if p[0] is not None:
                        deps.add(p[0])
                    deps.update(p[1])
            else:
                p = n["p"].get(key[1])
                if p is not None:
                    if p[0] is not None:
                        deps.add(p[0])
                    deps.update(p[1])
        return deps

    def add(self, eng, fn, reads=(), writes=(), dma=False, ss=True):
        op = _Op(len(self.ops), eng, fn, dma)
        if self.muted:
            return op
        if ss is not True:
            relaxed = set(ss)
            strict = self._collect([k for k in reads if k[0] not in relaxed], [k for k in writes if k[0] not in relaxed])
            loose = self._collect([k for k in reads if k[0] in relaxed], [k for k in writes if k[0] in relaxed])
            op.skip = {d for d in (loose - strict) if d.eng == eng and not d.dma}
        deps = op.deps
        for key in reads:
            n = self._node(key[0])
            if n["w"][0] is not None:
                deps.add(n["w"][0])
            if len(key) == 1:
                for p in n["p"].values():
                    if p[0] is not None:
                        deps.add(p[0])
            else:
                p = n["p"].get(key[1])
                if p is not None and p[0] is not None:
                    deps.add(p[0])
        op.raw = set(deps)
        for key in writes:
            n = self._node(key[0])
            if n["w"][0] is not None:
                deps.add(n["w"][0])
            deps.update(n["w"][1])
            if len(key) == 1:
                for p in n["p"].values():
                    if p[0] is not None:
                        deps.add(p[0])
                    deps.update(p[1])
            else:
                p = n["p"].get(key[1])
                if p is not None:
                    if p[0] is not None:
                        deps.add(p[0])
                    deps.update(p[1])
        for key in reads:
            n = self._node(key[0])
            if len(key) == 1:
                self._addreader(n["w"][1], op)
            else:
                p = n["p"].setdefault(key[1], [None, []])
                self._addreader(p[1], op)
        for key in writes:
            n = self._node(key[0])
            if len(key) == 1:
                n["w"] = [op, []]
                n["p"] = {}
            else:
                n["p"][key[1]] = [op, []]
        deps.discard(op)
        self.ops.append(op)
        self.last[eng] = op
        if dma:
            self._dmas.append(op)
        return op

    def barrier(self):
        lasts = list(self.last.values())
        dmas = self._dmas
        self._dmas = []
        for eng in self.ENGS:
            op = self.add(eng, None)
            op.deps.update(lasts)
            op.deps.update(dmas)
            op.deps.discard(op)
            op.raw = set(op.deps)

    def emit(self, nc, stack):
        ops = self.ops
        for op in ops:
            for d in op.deps:
                if (not d.dma) and (d.eng != op.eng or (op.eng != "pe" and d not in op.skip)):
                    d.signal = True
        cnt = {e: 0 for e in self.ENGS}
        for op in ops:
            if op.signal:
                op.sidx = cnt[op.eng]
                cnt[op.eng] += 1
        sems = {e: [stack.enter_context(nc.semaphore(f"s_{e}{k}")) for k in range(self.NSEM)] for e in self.ENGS}
        dsems = [stack.enter_context(nc.semaphore(f"d_{k}")) for k in range(self.NDMA)]
        uses = [0] * self.NDMA
        lastd = [None] * self.NDMA
        half = self.NDMA // 2
        nxt = {"pool": 0, "sp": 0}
        for op in ops:
            if op.dma:
                q = "pool" if op.eng == "pool" else "sp"
                k = nxt[q] % half + (0 if q == "pool" else half)
                nxt[q] += 1
                op.dsem = k
                uses[k] += 1
                op.dval = 16 * uses[k]
                op.dprev = lastd[k]
                lastd[k] = op

        def sig(op):
            ep, r = divmod(op.sidx, self.EPOCH)
            return sems[op.eng][ep % self.NSEM], (ep // self.NSEM) * self.EPOCH + r + 1

        by_eng = {e: [o for o in ops if o.eng == e] for e in self.ENGS}
        nwaits = [0]

        def run(eng_name, e):
            waited_eng = {}
            waited_dma = {}
            for op in by_eng[eng_name]:
                need_eng = {}
                for d in op.deps:
                    if d.dma:
                        if waited_dma.get(d.dsem, 0) < d.dval:
                            waited_dma[d.dsem] = d.dval
                            e.wait_ge(dsems[d.dsem], d.dval)
                            nwaits[0] += 1
                    elif d.eng != eng_name or (eng_name != "pe" and d not in op.skip):
                        if d.sidx > need_eng.get(d.eng, -1):
                            need_eng[d.eng] = d.sidx
                if op.dma and op.dprev is not None:
                    d = op.dprev
                    if waited_dma.get(d.dsem, 0) < d.dval:
                        waited_dma[d.dsem] = d.dval
                        e.wait_ge(dsems[d.dsem], d.dval)
                for pe_, sidx in need_eng.items():
                    if sidx > waited_eng.get(pe_, -1):
                        waited_eng[pe_] = sidx
                        ep, r = divmod(sidx, self.EPOCH)
                        e.wait_ge(sems[pe_][ep % self.NSEM], (ep // self.NSEM) * self.EPOCH + r + 1)
                        nwaits[0] += 1
                if op.fn is None:
                    if op.signal:
                        s_, v_ = sig(op)
                        e.nop().then_inc(s_, 1)
                    continue
                ins = op.fn(e)
                if op.dma:
                    ins.then_inc(dsems[op.dsem], 16)
                elif op.signal:
                    s_, v_ = sig(op)
                    ins.then_inc(s_, 1)

        with nc.Block() as block:
            @block.tensor
            def _(e):
                run("pe", e)

            @block.scalar
            def _(e):
                run("act", e)

            @block.vector
            def _(e):
                run("dve", e)

            @block.gpsimd
            def _(e):
                run("pool", e)

            @block.sync
            def _(e):
                run("sp", e)
        return dict(n_ops=len(ops), n_waits=nwaits[0], signals=dict(cnt))


def _tile_w(W, cols):
    K = W.shape[0] // 128
    sub = W[:, cols]
    return np.ascontiguousarray(sub.reshape(K, 128, len(cols)).transpose(1, 0, 2).reshape(128, K * len(cols)))


class _Blob:
    def __init__(self):
        self.parts, self.off, self.n = [], {}, 0

    def add(self, name, arr):
        arr = np.asarray(arr, dtype=np.float32)
        assert arr.shape[0] == 128
        self.off[name] = (self.n, arr.shape[1])
        self.parts.append(arr)
        self.n += arr.shape[1]

    def build(self):
        return np.ascontiguousarray(np.concatenate(self.parts, axis=1))


def _ar(a, b):
    return np.arange(a, b)


def _weight_specs():
    sp = []
    for j in range(48):
        sp.append((f"ada{j}", "ada_w", _ar(j * 128, (j + 1) * 128)))
    for h in range(4):
        sp.append((f"qa{h}", "w_in", _ar(O_QA + h * 128, O_QA + (h + 1) * 128)))
        sp.append((f"ka{h}", "w_in", _ar(O_KA + h * 128, O_KA + (h + 1) * 128)))
        sp.append((f"va{h}", "w_in", _ar(O_VA + h * 128, O_VA + (h + 1) * 128)))
    for j in range(2):
        sp.append((f"cq{j}", "w_in", _ar(O_CQ + j * 128, O_CQ + (j + 1) * 128)))
    sp.append(("ckv", "w_in", _ar(O_CKV, O_CKV + 128)))
    sp.append(("kr", "w_in", _ar(O_KR - 64, O_KR + 32)))
    sp.append(("krp", "w_in", np.concatenate([_ar(O_KR - 64, O_KR), _ar(O_KR + 16, O_KR + 32), _ar(O_KR, O_KR + 16)])))
    for j in range(8):
        sp.append((f"z{j}", "w_in", _ar(O_Z + j * 128, O_Z + (j + 1) * 128)))
    for j in range(16):
        sp.append((f"xbc{j}", "w_in", _ar(O_XBC + j * 128, O_XBC + (j + 1) * 128)))
    sp.append(("dt", "w_in", _ar(O_DT, O_DT + 16)))
    for j in range(24):
        sp.append((f"g{j}", "w_in", _ar(O_G + j * 128, O_G + (j + 1) * 128)))
    for h in range(8):
        sp.append((f"uq{h}", "mla_w_uq", _ar(h * 96, h * 96 + 96)))
        sp.append((f"uqp{h}", "mla_w_uq", np.concatenate([_ar(h * 96, h * 96 + 64), _ar(h * 96 + 80, h * 96 + 96), _ar(h * 96 + 64, h * 96 + 80)])))
        sp.append((f"ukn{h}", "mla_w_ukv", _ar(h * 128, h * 128 + 64)))
    for pr in range(4):
        sp.append((f"ukv{pr}", "mla_w_ukv", np.concatenate([_ar(2 * pr * 128 + 64, 2 * pr * 128 + 128), _ar((2 * pr + 1) * 128 + 64, (2 * pr + 1) * 128 + 128)])))
    for i in range(8):
        sp.append((f"wa{i}", "w_branch_a", _ar(i * 128, (i + 1) * 128)))
        sp.append((f"wb{i}", "w_branch_b", _ar(i * 128, (i + 1) * 128)))
        sp.append((f"wc{i}", "w_branch_c", _ar(i * 128, (i + 1) * 128)))
        sp.append((f"wo{i}", "w_out", _ar(i * 128, (i + 1) * 128)))
        sp.append((f"dn{i}", "ffn_w_down", _ar(i * 128, (i + 1) * 128)))
    for i in range(NF):
        sp.append((f"ua{i}", "ffn_w_up", _ar(i * 128, (i + 1) * 128)))
        sp.append((f"uv{i}", "ffn_w_up", _ar(FFN + i * 128, FFN + (i + 1) * 128)))
    return sp


_WROWS = {"ada_w": 1024, "w_in": 1024, "mla_w_uq": 256, "mla_w_ukv": 128, "w_branch_a": 512, "w_branch_b": 512,
          "w_branch_c": 1024, "w_out": 1024, "ffn_w_down": FFN, "ffn_w_up": 1024}


def _weight_offsets():
    off, n = {}, 0
    for name, src, cols in _weight_specs():
        w = (_WROWS[src] // 128) * len(cols)
        off[name] = (n, w)
        n += w
    return off, n


def _weight_blob(inp, l):
    return np.ascontiguousarray(np.concatenate([_tile_w(inp[src][l], cols) for _, src, cols in _weight_specs()], axis=1))


def _col(v, n=128):
    v = np.asarray(v, np.float32)
    return np.ascontiguousarray(v.reshape(-1, n).T)


def _small_blob(inp, l):
    b = _Blob()
    b.add("ada_b", _col(inp["ada_b"][l]))
    b.add("norm_mix", _col(inp["norm_mix"][l]))
    b.add("norm_ffn", _col(inp["norm_ffn"][l]))
    b.add("qn", np.tile(inp["da_q_norm"][l], 2)[:, None])
    b.add("kn", np.tile(inp["da_k_norm"][l], 2)[:, None])
    b.add("subln", inp["da_subln"][l][:, None])
    b.add("lam", np.broadcast_to(inp["da_lambda"][l].reshape(1, 256), (128, 256)))
    b.add("qan", _col(inp["mla_q_a_norm"][l]))
    b.add("kvan", _col(inp["mla_kv_a_norm"][l]))
    qn = np.zeros((128, 2), np.float32)
    w = inp["mla_q_norm"][l]
    qn[:96, 0] = w
    qn[64:96, 1] = np.concatenate([w[80:96], w[64:80]])
    b.add("mqn", qn)
    kn = np.zeros((128, 2), np.float32)
    w = inp["mla_k_norm"][l]
    kn[:96, 0] = w
    kn[64:96, 1] = np.concatenate([w[80:96], w[64:80]])
    b.add("mkn", kn)
    cw = inp["ssm_conv_w"][l]
    b.add("scw", np.concatenate([_col(cw[k]) for k in range(4)], axis=1).reshape(128, 4, 16).transpose(0, 2, 1).reshape(128, 64))
    b.add("scb", _col(inp["ssm_conv_b"][l]))
    b.add("dtb", np.broadcast_to(np.tile(inp["ssm_dt_bias"][l], 16)[None, :], (128, 256)))
    b.add("alog", np.broadcast_to(np.tile(inp["ssm_a_log"][l], 16)[None, :], (128, 256)))
    b.add("ssd_d", np.repeat(inp["ssm_d"][l], 64).reshape(8, 128).T)
    b.add("ssn", _col(inp["ssm_norm"][l]))
    fw = inp["ffn_conv_w"][l]
    b.add("fcw", np.concatenate([_col(fw[k]) for k in range(3)], axis=1).reshape(128, 3, 44).transpose(0, 2, 1).reshape(128, 132))
    b.add("fcb", _col(inp["ffn_conv_b"][l]))
    return b


def _t5_bucket_np(d):
    d = np.maximum(d, 0)
    large = 16 + (np.log(np.maximum(d, 1).astype(np.float32) / 16) / math.log(128 / 16) * 16).astype(np.int32)
    large = np.minimum(large, 31)
    return np.where(d < 16, d, large)


def _const_blob():
    b = _Blob()
    b.add("ident", np.eye(128, dtype=np.float32))
    b.add("ones", np.ones((128, 128), np.float32))
    blk = np.zeros((128, 128), np.float32)
    blk[:64, :64] = 1
    blk[64:, 64:] = 1
    b.add("blk64", blk)
    b.add("tri", np.triu(np.ones((128, 128), np.float32)))
    b.add("utri", np.tril(np.ones((128, 128), np.float32), -1))
    sel = np.zeros((128, 512), np.float32)
    for h in range(4):
        sel[h, h * 128:(h + 1) * 128] = 1
    b.add("sel4", sel)
    kk = np.arange(128)[:, None]
    u = np.arange(128)[None, :]
    b.add("maskb", np.where(u >= kk, 0.0, NEG).astype(np.float32))
    oh = np.zeros((128, 384), np.float32)
    m = np.arange(384)
    d = m - 127
    bk = _t5_bucket_np(d)
    for mm in range(384):
        if d[mm] < 0:
            oh[32, mm] = 1
        else:
            oh[bk[mm], mm] = 1
    b.add("oh", oh)
    invf = np.zeros((128, 2), np.float32)
    f = (10000.0 ** (-np.arange(0, 32, 2, dtype=np.float32) / 32)).astype(np.float32)
    invf[64:96, 0] = np.concatenate([f, f])
    invf[64:80, 1] = -1
    invf[80:96, 1] = 1
    b.add("invf", invf)
    return b


_SHAPES = {"ada_b": (NL, 6144), "norm_mix": (NL, D), "norm_ffn": (NL, D), "da_q_norm": (NL, 64), "da_k_norm": (NL, 64),
           "da_subln": (NL, 128), "da_lambda": (NL, 4, 64), "mla_q_a_norm": (NL, 256), "mla_kv_a_norm": (NL, 128),
           "mla_q_norm": (NL, 96), "mla_k_norm": (NL, 96), "ssm_conv_w": (NL, 4, 2048), "ssm_conv_b": (NL, 2048),
           "ssm_dt_bias": (NL, 16), "ssm_a_log": (NL, 16), "ssm_d": (NL, 16), "ssm_norm": (NL, D),
           "ffn_conv_w": (NL, 3, 2 * FFN), "ffn_conv_b": (NL, 2 * FFN)}


class _StopBuild(Exception):
    pass


def build(nlayers=NL, taps=(), do_mixer=True, do_ffn=True, branches=("ssd", "da", "mla"), dbg=None):
    dbg = dbg or {}
    taps = set(taps)
    nc = bass.Bass("TRN2", target_bir_lowering=False)
    P = Prog()
    woff, WTOT = _weight_offsets()
    dummy = {k: np.zeros(v, np.float32) for k, v in _SHAPES.items()}
    sb_ = _small_blob(dummy, 0)
    smoff, SMTOT = sb_.off, sb_.n
    cb_ = _const_blob()
    coff, CTOT = cb_.off, cb_.n

    def dr(name, shape, dt, kind):
        return nc.dram_tensor(name, shape, dt, kind=kind).ap()

    xT_d = dr("xT", [D, S], F32, "ExternalInput")
    cst_d = dr("cst", [128, CTOT], F32, "ExternalInput")
    sm_d = dr("sm", [NL, 128, SMTOT], F32, "ExternalInput")
    cv_d = dr("cvec", [128, 8], F32, "ExternalInput")
    pos_d = dr("pos", [32, S], I32, "ExternalInput")
    relb_d = dr("relb", [32, 4], F32, "ExternalInput")
    wb_d = dr("wblob", [NL, 128, WTOT], F32, "ExternalInput")
    out_d = dr("outT", [D, S], F32, "ExternalOutput")
    zscr = dr("zscr", [4, 128, 384], F32, "Internal")
    yscr = dr("yscr", [8, 128, S], BF16, "Internal")
    tap_d = {}

    with ExitStack() as st:
        tcount = [0]

        def T(name, shape, dt, stk=None):
            tcount[0] += 1
            return (stk or st).enter_context(nc.sbuf_tensor(f"t{tcount[0]}_{name}", shape, dt))

        ps = [st.enter_context(nc.psum_tensor(f"ps{i}", [128, 512], F32)) for i in range(7)]

        def MM(out, lhsT, rhs, start, stop, r, w):
            P.add("pe", lambda e: e.matmul(out, lhsT, rhs, start=start, stop=stop), r, w)

        def ACT(out, in_, func, r, w, bias=None, scale=None, ss=True):
            kw = {}
            if bias is not None:
                kw["bias"] = bias
            if scale is not None:
                kw["scale"] = scale
            P.add("act", lambda e: e.activation(out, in_, func, **kw), r, w, ss=ss)

        def TT(out, in0, in1, op, r, w, eng="dve", ss=True):
            P.add(eng, lambda e: e.tensor_tensor(out, in0, in1, op), r, w, ss=ss)

        def TS(out, in0, s1, s2, op0, op1, r, w, eng="dve"):
            if s2 is None:
                P.add(eng, lambda e: e.tensor_scalar(out, in0, s1, None, op0), r, w)
            else:
                P.add(eng, lambda e: e.tensor_scalar(out, in0, s1, s2, op0, op1), r, w)

        def STT(out, in0, sc, in1, op0, op1, r, w, eng="dve", ss=True):
            P.add(eng, lambda e: e.scalar_tensor_tensor(out, in0, sc, in1, op0, op1), r, w, ss=ss)

        def CP(out, in_, r, w, eng="dve"):
            P.add(eng, lambda e: e.tensor_copy(out, in_), r, w)

        def DMA(q, out, in_, r, w):
            return P.add(q, lambda e: e.dma_start(out=out, in_=in_), r, w, dma=True)

        class Rot:
            def __init__(self, name, items):
                self.name, self.items, self.i = name, items, 0

            def next(self):
                k = self.i % len(self.items)
                self.i += 1
                return self.items[k], (self.name, k)

        def tap(name, src_ap, shape, dt, keys):
            if name not in taps:
                return
            d = dr("tap_" + name, shape, dt, "ExternalOutput")
            tap_d[name] = d
            DMA("sp", d, src_ap, keys, [("tapout", name)])

        x = T("x_sb", [128, 8, S], F32)
        hT = T("hT", [128, 8, S], BF16)
        cstf = T("cstf", [128, 640], F32)
        ones_f = cstf[:, 0:128]
        tri_f = cstf[:, 128:256]
        utri_f = cstf[:, 256:384]
        maskb = cstf[:, 384:512]
        ident_f = cstf[:, 512:640]
        cstb = T("cstb", [128, 640], BF16)
        ident_b = cstb[:, 0:128]
        ones_b = cstb[:, 128:256]
        blk64_b = cstb[:, 256:384]
        tri_b = cstb[:, 384:512]
        utri_b = cstb[:, 512:640]
        sm = T("sm_sb", [128, SMTOT], F32)
        cosT = T("cosT", [128, S], BF16)
        sinS = T("sinS", [128, S], BF16)
        band = T("band", [128, 4, 256], F32)
        cbias = T("cbias", [128, 4], F32)
        modt = T("modt", [128, NL * 48], F32)
        lsc = T("lsc", [128, 64], F32)
        wsl = [T(f"wsl{i}", [128, 1024], BF16) for i in range(3)]
        wrot = Rot("w", wsl)

        def smc(name, a=0, b=None):
            o, n = smoff[name]
            return sm[:, o + a:o + (n if b is None else b)]

        def wload(l, name, rot=None):
            o, n = woff[name]
            (buf, key) = (rot or wrot).next()
            DMA("pool", buf[:, 0:n], wb_d[l, :, o:o + n], [], [key])
            return buf, key

        def c_(name):
            o, n = coff[name]
            return cst_d[:, o:o + n]

        DMA("sp", cstf[:, 0:128], c_("ones"), [], [("cstf",)])
        DMA("sp", cstf[:, 128:256], c_("tri"), [], [("cstf",)])
        DMA("sp", cstf[:, 256:384], c_("utri"), [], [("cstf",)])
        DMA("sp", cstf[:, 384:512], c_("maskb"), [], [("cstf",)])
        DMA("sp", cstf[:, 512:640], c_("ident"), [], [("cstf",)])
        DMA("pool", cstb[:, 0:128], c_("ident"), [], [("cstb",)])
        DMA("pool", cstb[:, 128:256], c_("ones"), [], [("cstb",)])
        DMA("pool", cstb[:, 256:384], c_("blk64"), [], [("cstb",)])
        DMA("pool", cstb[:, 384:512], c_("tri"), [], [("cstb",)])
        DMA("pool", cstb[:, 512:640], c_("utri"), [], [("cstb",)])
        for c in range(8):
            DMA("sp", x[:, c, :], xT_d[c * 128:(c + 1) * 128, :], [], [("x",)])

        with ExitStack() as s0:
            oh = T("oh", [128, 384], F32, s0)
            invf = T("invf", [128, 2], F32, s0)
            DMA("sp", oh[:], c_("oh"), [], [("oh",)])
            DMA("sp", invf[:], c_("invf"), [], [("invf",)])
            pos_i = T("pos_i", [128, S], I32, s0)
            ang = T("ang", [128, S], F32, s0)
            tq = T("tq", [128, S], F32, s0)
            tk = T("tk", [128, S], F32, s0)
            ti = T("ti", [128, S], I32, s0)
            R_ = slice(64, 96)
            DMA("sp", pos_i[R_, :], pos_d, [], [("pos_i",)])
            CP(ang[R_, :], pos_i[R_, :], [("pos_i",)], [("ang",)])
            TS(ang[R_, :], ang[R_, :], invf[R_, 0:1], None, ALU.mult, None, [("ang",), ("invf",)], [("ang",)])
            for which, shift, dst in ((0, 0.0, sinS), (1, math.pi / 2, cosT)):
                TS(tq[R_, :], ang[R_, :], shift, None, ALU.add, None, [("ang",)], [("tq",)])
                TS(tk[R_, :], tq[R_, :], 1.0 / (2 * math.pi), None, ALU.mult, None, [("tq",)], [("tk",)])
                CP(ti[R_, :], tk[R_, :], [("tk",)], [("ti",)])
                CP(tk[R_, :], ti[R_, :], [("ti",)], [("tk",)])
                STT(tq[R_, :], tk[R_, :], -2 * math.pi, tq[R_, :], ALU.mult, ALU.add, [("tk",), ("tq",)], [("tq",)])
                TS(tk[R_, :], tq[R_, :], math.pi, 2 * math.pi, ALU.is_gt, ALU.mult, [("tq",)], [("tk",)])
                TT(tq[R_, :], tq[R_, :], tk[R_, :], ALU.subtract, [("tq",), ("tk",)], [("tq",)])
                TS(tk[R_, :], tq[R_, :], -math.pi, 2 * math.pi, ALU.is_lt, ALU.mult, [("tq",)], [("tk",)])
                TT(tq[R_, :], tq[R_, :], tk[R_, :], ALU.add, [("tq",), ("tk",)], [("tq",)])
                TS(tq[R_, :], tq[R_, :], math.pi, -math.pi, ALU.min, ALU.max, [("tq",)], [("tq",)])
                ACT(tk[R_, :], tq[R_, :], AF.Sin, [("tq",)], [("tk",)])
                if which == 0:
                    TS(dst[R_, :], tk[R_, :], invf[R_, 1:2], None, ALU.mult, None, [("tk",), ("invf",)], [("sinS",)])
                else:
                    CP(dst[R_, :], tk[R_, :], [("tk",)], [("cosT",)])
            relb = T("relb", [128, 4], F32, s0)
            tbh = T("tbh", [128, 128], F32, s0)
            gb = T("gb", [128, 384], F32, s0)
            DMA("sp", relb[0:32, :], relb_d, [], [("relb",)])
            P.add("dve", lambda e: e.memset(relb[32:33, :], NEG), [], [("relb",)])
            for h in range(4):
                TS(tbh[0:33, :], ones_f[0:33, :], relb[0:33, h:h + 1], None, ALU.mult, None, [("cstf",), ("relb",)], [("tbh",)])
                MM(ps[h % 2][:, 0:384], tbh[0:33, :], oh[0:33, :], True, True, [("tbh",), ("oh",)], [("ps", h % 2)])
                CP(gb[:], ps[h % 2][:, 0:384], [("ps", h % 2)], [("gb",)])
                CP(cbias[:, h:h + 1], gb[:, 382:383], [("gb",)], [("cbias",)])
                DMA("sp", zscr[h], gb[:], [("gb",)], [("zscr", h)])
                DMA("sp", band[:, h, :], bass.AP(zscr.tensor, h * 128 * 384 + 127, [[383, 128], [1, 256]]), [("zscr", h)], [("band",)])
            cv = T("cv", [128, 8], F32, s0)
            cact = T("cact", [128, 8], BF16, s0)
            DMA("sp", cv[:], cv_d, [], [("cv",)])
            ACT(cact[:], cv[:], AF.Silu, [("cv",)], [("cact",)])
            for l in range(nlayers):
                DMA("sp", sm[:], sm_d[l], [], [("sm",)])
                b = 2 + (l % 2)
                for j in range(48):
                    wt, wk = wload(l, f"ada{j}")
                    for k in range(8):
                        MM(ps[b][:, j:j + 1], wt[:, k * 128:(k + 1) * 128], cact[:, k:k + 1], k == 0, k == 7, [wk, ("cact",)], [("ps", b)])
                TT(modt[:, l * 48:(l + 1) * 48], ps[b][:, 0:48], smc("ada_b"), ALU.add, [("ps", b), ("sm",)], [("modt",)])
        P.barrier()

        prot = Rot("ps", [0, 1, 2])

        class Rot6:
            banks = [0, 1, 2, 4, 5, 6]

            def __init__(self):
                self.i = 0

            def next(self):
                b = self.banks[self.i % 6]
                self.i += 1
                return b, ("ps", b)

        prot6 = Rot6()

        def rstd_chain(out_sb, ss_ps, inv, r, w):
            ACT(out_sb, ss_ps, AF.Ln, r, w, bias=EPS, scale=inv)
            ACT(out_sb, out_sb, AF.Exp, w, w, scale=-0.5)

        def norm_to_hT(Gc, SHc, stk):
            xsq = T("xsq", [128, 8, 512], BF16, stk)
            rs = T("nrs", [128, 512], F32, stk)
            ntmp = [T(f"ntmp{i}", [128, 512], F32, stk) for i in range(2)]
            for tc in range(NTC):
                ts_ = slice(tc * 512, (tc + 1) * 512)
                for c in range(8):
                    if c % 2 == 0:
                        ACT(xsq[:, c, :], x[:, c, ts_], AF.Square, [("x", tc)], [("xsq", c)])
                    else:
                        TT(xsq[:, c, :], x[:, c, ts_], x[:, c, ts_], ALU.mult, [("x", tc)], [("xsq", c)])
                for c in range(8):
                    MM(ps[3][:], ones_b, xsq[:, c, :], c == 0, c == 7, [("xsq", c), ("cstb",)], [("ps", 3)])
                rstd_chain(rs[:], ps[3][:], 1.0 / D, [("ps", 3)], [("nrs",)])
                for c in range(8):
                    TT(ntmp[c % 2][:], x[:, c, ts_], rs[:], ALU.mult, [("x", tc), ("nrs",)], [("ntmp", c % 2)])
                    ACT(hT[:, c, ts_], ntmp[c % 2][:], AF.Identity, [("ntmp", c % 2), ("lsc",)], [("hT", tc)],
                        bias=SHc[:, c:c + 1], scale=Gc[:, c:c + 1], ss=("hT",))

        def conv_tc(ps_ap, pre, acc, nt, wcols, bcol, first, pk, ak, rps, halo=None, hk=None):
            H = nt - 1
            if first:
                P.add("dve", lambda e: e.memset(pre[:, 0:H], 0.0), [], [pk])
            elif halo is not None:
                CP(pre[:, 0:H], halo, [hk], [pk])
            ACT(pre[:, H:H + 512], ps_ap, AF.Identity, rps, [pk])
            ACT(acc, ps_ap, AF.Identity, rps + [("sm",)], [ak], bias=bcol, scale=wcols[nt - 1])
            for k in range(nt - 2, -1, -1):
                STT(acc, pre[:, k:k + 512], wcols[k], acc, ALU.mult, ALU.add, [pk, ak, ("sm",)], [ak])
            if halo is not None:
                CP(halo, pre[:, 512:512 + H], [pk], [hk])
            else:
                CP(pre[:, 0:H], pre[:, 512:512 + H], [pk], [pk])

        def conv_A(ps_ap, pre, acc, nt, wcols, bcol, pk, ak, rps):
            H = nt - 1
            ACT(pre[:, H:H + 512], ps_ap, AF.Identity, rps, [pk])
            ACT(acc, ps_ap, AF.Identity, rps + [("sm",)], [ak], bias=bcol, scale=wcols[nt - 1])

        def conv_B(pre, acc, nt, wcols, first, pk, ak, halo, hk):
            H = nt - 1
            if first:
                P.add("dve", lambda e: e.memset(pre[:, 0:H], 0.0), [], [pk])
            else:
                CP(pre[:, 0:H], halo, [hk], [pk])
            for k in range(nt - 2, -1, -1):
                STT(acc, pre[:, k:k + 512], wcols[k], acc, ALU.mult, ALU.add, [pk, ak, ("sm",)], [ak])
            CP(halo, pre[:, 512:512 + H], [pk], [hk])

        def attn_chunk(c, KT, QT, prow, kq_keys, V_get, vkey, bandap, BW, cb_, scale, bo, bs, Erot, tmpb, hook=None):
            LA = 2
            q0 = c * 512
            nk = 4 * c + 4
            pend = []
            for j in range(nk + LA):
                if j < nk:
                    k0 = j * 128
                    qs = max(q0, k0)
                    n = q0 + 512 - qs
                    col0 = qs - q0
                    b, bk = prot.next()
                    MM(ps[b][:, 0:n], KT[prow, k0:k0 + 128], QT[prow, qs:qs + n], True, True, kq_keys, [bk])
                    (E, ek) = Erot.next()
                    u0 = qs - k0
                    nband = max(0, min(n, BW - u0))
                    if nband > 0:
                        tb_, tbk = tmpb.next()
                        STT(tb_[:, 0:nband], ps[b][:, 0:nband], scale, bandap[:, u0:u0 + nband], ALU.mult, ALU.add,
                            [bk, ("band",), ("cstf",)], [tbk])
                        ACT(E[:, 0:nband], tb_[:, 0:nband], AF.Exp, [tbk], [ek])
                    if n > nband:
                        ACT(E[:, nband:n], ps[b][:, nband:n], AF.Exp, [bk, ("cbias",)], [ek], bias=cb_, scale=scale, ss=("E",))
                    pend.append((j, E, ek, n, col0))
                    if hook is not None:
                        hooks = hook if isinstance(hook, (list, tuple)) else [(2, hook)]
                        for (st_, fn_) in hooks:
                            if j == min(st_, nk - 1):
                                fn_()
                if j >= LA:
                    jj, E, ek, n, col0 = pend.pop(0)
                    MM(ps[bo][:, col0:col0 + n], V_get(jj), E[:, 0:n], jj == 0, jj == nk - 1, [vkey, ek], [("ps", bo)])
                    MM(ps[bs][:, col0:col0 + n], ones_b, E[:, 0:n], jj == 0, jj == nk - 1, [ek, ("cstb",)], [("ps", bs)])

        def merge_branch(l, yT, ykey, KX, wname, gbase, gm):
            with ExitStack() as sk:
                mg = T("mg", [128, 8, 1024], BF16, sk)
                sg = [T(f"sg{i}", [128, 512], F32, sk) for i in range(2)]
                wmr = Rot("wm", [T(f"wm{i}", [128, 1024], BF16, sk) for i in range(6)])
                yh = T("yh", [128, 8, 1024], BF16, sk) if yT is None else None
                for half in range(2):
                    if yT is None:
                        for k in range(8):
                            DMA("sp", yh[:, k, :], yscr[k, :, half * 1024:(half + 1) * 1024], [("yscr",)], [("yh",)])
                    for i in range(8):
                        wb, wbk = wload(l, f"{wname}{i}", wmr)
                        wg, wgk = wload(l, f"g{gbase + i}", wmr)
                        for t2 in range(2):
                            tc = half * 2 + t2
                            ts_ = slice(tc * 512, (tc + 1) * 512)
                            b1, k1 = prot6.next()
                            for k in range(KX):
                                rhs_ = yh[:, k, t2 * 512:(t2 + 1) * 512] if yT is None else yT[:, k, ts_]
                                MM(ps[b1][:], wb[:, k * 128:(k + 1) * 128], rhs_, k == 0, k == KX - 1, [wbk, ("yh",) if yT is None else ykey], [k1])
                            b2, k2 = prot6.next()
                            for k in range(8):
                                MM(ps[b2][:], wg[:, k * 128:(k + 1) * 128], hT[:, k, ts_], k == 0, k == 7, [wgk, ("hT", tc)], [k2])
                            ACT(sg[t2][:], ps[b2][:], AF.Sigmoid, [k2], [("sg", t2)])
                            TT(mg[:, i, t2 * 512:(t2 + 1) * 512], ps[b1][:], sg[t2][:], ALU.mult, [k1, ("sg", t2)], [("mg", t2)], ss=("mg",))
                    for io in range(8):
                        wo, wok = wload(l, f"wo{io}", wmr)
                        for t2 in range(2):
                            tc = half * 2 + t2
                            ts_ = slice(tc * 512, (tc + 1) * 512)
                            b, bk = prot6.next()
                            for k in range(8):
                                MM(ps[b][:], wo[:, k * 128:(k + 1) * 128], mg[:, k, t2 * 512:(t2 + 1) * 512], k == 0, k == 7, [wok, ("mg", t2)], [bk])
                            STT(x[:, io, ts_], ps[b][:], gm[:, io:io + 1], x[:, io, ts_], ALU.mult, ALU.add, [bk, ("x", tc), ("modt",)], [("x", tc)], ss=("x",))
            P.barrier()


        def mixer(l, g_m, neglam, sublnw):
            X_ = mybir.AxisListType.X
            if "ssd" in branches:
                with ExitStack() as sk:
                    ysT = T("ysT", [128, 8, S], BF16, sk)
                    with ExitStack() as s2:
                        pst = s2.enter_context(nc.psum_tensor(f"pst{l}", [128, 1024], BF16))
                        dtt, da, negcs, cd, dtend = [T(n_, [128, 256], F32, s2) for n_ in ("dtt", "da", "negcs", "cd", "dtend")]
                        scr = T("scr", [128, 1032], F32, s2)
                        cscol, csl, at_, e1 = scr[:, 0:256], scr[:, 256:512], scr[:, 512:768], scr[:, 768:1024]
                        halo4 = T("halo4", [128, 4, 4], F32, s2)
                        da_hi = T("da_hi", [128, 256], BF16, s2)
                        da_lo = T("da_lo", [128, 256], BF16, s2)
                        wdt, wdk = wload(l, "dt")
                        for tt in range(16):
                            for k in range(8):
                                MM(ps[4][:, tt * 16:(tt + 1) * 16], hT[:, k, tt * 128:(tt + 1) * 128], wdt[:, k * 16:(k + 1) * 16], k == 0, k == 7,
                                   [wdk, ("hT", tt // 4)], [("ps", 4)])
                        TT(e1[:], ps[4][:, 0:256], smc("dtb"), ALU.add, [("ps", 4), ("sm",)], [("e1",)])
                        ACT(e1[:], e1[:], AF.Exp, [("e1",)], [("e1",)])
                        ACT(dtt[:], e1[:], AF.Ln, [("e1",)], [("dtt",)], bias=1.0, scale=1.0)
                        ACT(at_[:], smc("alog"), AF.Exp, [("sm",)], [("at",)])
                        STT(da[:], at_[:], -1.0, dtt[:], ALU.mult, ALU.mult, [("at",), ("dtt",)], [("da",)])
                        ACT(da_hi[:], da[:], AF.Identity, [("da",)], [("da_hi",)])
                        ACT(e1[:], da_hi[:], AF.Identity, [("da_hi",)], [("e1",)])
                        TT(da_lo[:], da[:], e1[:], ALU.subtract, [("da",), ("e1",)], [("da_lo",)])
                        for tt in range(16):
                            MM(ps[5][:, tt * 16:(tt + 1) * 16], tri_f, da[:, tt * 16:(tt + 1) * 16], True, True, [("cstf",), ("da",)], [("ps", 5)])
                            MM(ps[6][:, tt * 16:(tt + 1) * 16], ones_f, da[:, tt * 16:(tt + 1) * 16], True, True, [("cstf",), ("da",)], [("ps", 6)])
                        CP(cscol[:], ps[5][:, 0:256], [("ps", 5)], [("cscol",)])
                        TS(negcs[:], ps[5][:, 0:256], -1.0, None, ALU.mult, None, [("ps", 5)], [("negcs",)])
                        CP(csl[:], ps[6][:, 0:256], [("ps", 6)], [("csl",)])
                        ACT(cd[:], ps[6][:, 0:256], AF.Exp, [("ps", 6)], [("cd",)])
                        TT(e1[:], csl[:], cscol[:], ALU.subtract, [("csl",), ("cscol",)], [("e1",)])
                        ACT(e1[:], e1[:], AF.Exp, [("e1",)], [("e1",)])
                        TT(dtend[:], e1[:], dtt[:], ALU.mult, [("e1",), ("dtt",)], [("dtend",)])
                        scw, scb, dcol, ssn = smc("scw"), smc("scb"), smc("ssd_d"), smc("ssn")
                        P.barrier()
                        if dbg.get("ssd_stop") == 1:
                            P.muted = True
                        for g in dbg.get("ssd_groups", range(4)):
                            with ExitStack() as s3:
                                xs_fm = T("xs_fm", [128, 2, S], BF16, s3)
                                bm_fm = T("bm_fm", [128, S], BF16, s3)
                                cm_fm = T("cm_fm", [128, S], BF16, s3)
                                pre = T("spre", [128, 515], F32, s3)
                                acc = T("sacc", [128, 512], F32, s3)
                                streams = ((2 * g, xs_fm[:, 0, :], ("xsfm", 0)), (2 * g + 1, xs_fm[:, 1, :], ("xsfm", 1)),
                                           (8 + g, bm_fm[:, :], ("bmfm",)), (12 + g, cm_fm[:, :], ("cmfm",)))
                                cbufs = ((pre, acc[:], ("spre", 0), ("sacc", 0)), (scr[:, 0:515], scr[:, 516:1028], ("spre", 1), ("sacc", 1)))
                                def sconv2(ctx):
                                    j, dst, dk, tc, si, sidx = ctx
                                    pre_, acc_, pk_, ak_ = cbufs[si]
                                    conv_B(pre_, acc_, 4, [scw[:, j * 4 + k:j * 4 + k + 1] for k in range(4)], tc == 0, pk_, ak_,
                                           halo4[:, sidx, 0:3], ("halo4", sidx))
                                    ACT(dst[:, tc * 512:(tc + 1) * 512], acc_, AF.Silu, [ak_], [dk], ss=(dk[0],))

                                for pair in (streams[0:2], streams[2:4]):
                                    wts = [wload(l, f"xbc{j}") for (j, _, _) in pair]
                                    prev_ = None
                                    for tc in range(NTC):
                                        ts_ = slice(tc * 512, (tc + 1) * 512)
                                        for si, (j, dst, dk) in enumerate(pair):
                                            wt, wk = wts[si]
                                            pre_, acc_, pk_, ak_ = cbufs[si]
                                            sidx = [q[0] for q in streams].index(j)
                                            b, bk = prot.next()
                                            for k in range(8):
                                                MM(ps[b][:], wt[:, k * 128:(k + 1) * 128], hT[:, k, ts_], k == 0, k == 7, [wk, ("hT", tc)], [bk])
                                            conv_A(ps[b][:], pre_, acc_, 4, [scw[:, j * 4 + k:j * 4 + k + 1] for k in range(4)], scb[:, j:j + 1], pk_, ak_, [bk])
                                            if prev_ is not None:
                                                sconv2(prev_)
                                            prev_ = (j, dst, dk, tc, si, sidx)
                                    sconv2(prev_)
                                Hs = T("Hs", [128, 256], F32, s3)
                                prevb = T("prevb", [128, 256], BF16, s3)
                                ysb = T("ysb", [128, 2, 512], F32, s3)
                                gz, grs = acc, pre

                                def two(name, shape, dt_, n=2):
                                    ts_l = [T(f"{name}{i}", shape, dt_, s3) for i in range(n)]
                                    return [ts_l[i % n] for i in range(2)]

                                xdt2, xdte2, bmt2 = two("xdt", [128, 256], BF16), two("xdte", [128, 256], BF16, 1), two("bmt", [128, 128], BF16, 1)
                                cbm2, dec2 = two("cbm", [128, 128], F32, 1), two("dec", [128, 512], F32, 1)
                                GT2, ecs2, Cs2 = two("GT", [128, 512], BF16), two("ecs", [128, 512], F32, 1), two("Cs", [128, 512], BF16)
                                cmf2, xsf2 = two("cmf", [128, 128], F32, 1), two("xsf", [128, 2, 128], F32)
                                Rh2 = two("Rh", [128, 512], BF16)
                                Rl2 = two("Rl", [128, 512], BF16)

                                def build_R(c, g=g):
                                    p = c % 2
                                    col = c * 16 + 4 * g
                                    v4 = lambda ap_: ap_.rearrange("p (h l) -> p h l", h=4)
                                    tb4 = tri_b.unsqueeze(1).broadcast_to([128, 4, 128])
                                    TT(v4(Rh2[p][:]), tb4, da_hi[:, col:col + 4].unsqueeze(2).broadcast_to([128, 4, 128]), ALU.mult, [("cstb",), ("da_hi",)], [("Rh", p)], eng="pool")
                                    TT(v4(Rl2[p][:]), tb4, da_lo[:, col:col + 4].unsqueeze(2).broadcast_to([128, 4, 128]), ALU.mult, [("cstb",), ("da_lo",)], [("Rl", p)], eng="pool")
                                P.add("dve", lambda e, Hs=Hs: e.memset(Hs[:], 0.0), [], [("Hs",)])

                                def stageA(c, g=g, xs_fm=xs_fm, bm_fm=bm_fm, cm_fm=cm_fm):
                                    p = c % 2
                                    tsl = slice(c * 128, (c + 1) * 128)
                                    col = c * 16 + 4 * g
                                    xdt, xdte, bmt, cbm, dec = xdt2[p], xdte2[p], bmt2[p], cbm2[p], dec2[p]
                                    GT, ecs, Cs, cmf, xsf = GT2[p], ecs2[p], Cs2[p], cmf2[p], xsf2[p]
                                    v4 = lambda ap_: ap_.rearrange("p (h l) -> p h l", h=4)
                                    Rh, Rl = Rh2[p], Rl2[p]
                                    if c == 0:
                                        build_R(0)
                                    for jj in range(2):
                                        P.add("pe", lambda e, o_=pst[:, jj * 128:(jj + 1) * 128], i_=xs_fm[:, jj, tsl]: e.transpose(o_, i_, ident_b),
                                              [("xsfm", jj), ("cstb",)], [("pst",)])
                                    P.add("pe", lambda e, o_=pst[:, 256:384], i_=bm_fm[:, tsl]: e.transpose(o_, i_, ident_b), [("bmfm",), ("cstb",)], [("pst",)])
                                    MM(ps[4 + p][:, 256:384], bm_fm[:, tsl], cm_fm[:, tsl], True, True, [("bmfm",), ("cmfm",)], [("ps", 4 + p)])
                                    MM(ps[6][:], utri_b, Rh[:], True, False, [("cstb",), ("Rh", p)], [("ps", 6)])
                                    MM(ps[6][:], utri_b, Rl[:], False, True, [("cstb",), ("Rl", p)], [("ps", 6)])
                                    MM(ps[3][:], ones_b, Rh[:], True, False, [("cstb",), ("Rh", p)], [("ps", 3)])
                                    MM(ps[3][:], ones_b, Rl[:], False, True, [("cstb",), ("Rl", p)], [("ps", 3)])
                                    ACT(cmf[:], cm_fm[:, tsl], AF.Identity, [("cmfm",)], [("cmf", 0)])
                                    for jj in range(2):
                                        ACT(xsf[:, jj, :], xs_fm[:, jj, tsl], AF.Identity, [("xsfm", jj)], [("xsf", p)])
                                    for h in range(4):
                                        hs = slice(h * 64, (h + 1) * 64)
                                        ACT(xdte[:, hs], pst[:, hs], AF.Identity, [("pst",), ("dtend",)], [("xdte", h)], scale=dtend[:, col + h:col + h + 1])
                                    ACT(bmt[:], pst[:, 256:384], AF.Identity, [("pst",)], [("bmt", 0)])
                                    for h in range(4):
                                        hs = slice(h * 64, (h + 1) * 64)
                                        ACT(xdt[:, hs], pst[:, hs], AF.Identity, [("pst",), ("dtt",)], [("xdt", p * 4 + h)], scale=dtt[:, col + h:col + h + 1])
                                    MM(ps[4 + p][:, 0:256], bmt[:], xdte[:], True, True, [("bmt", 0), ("xdte",)], [("ps", 4 + p)])
                                    TT(cbm[:], ps[4 + p][:, 256:384], tri_f, ALU.mult, [("ps", 4 + p), ("cstf",)], [("cbm", 0)])
                                    ACT(dec[:], ps[6][:], AF.Exp, [("ps", 6)], [("dec",)])
                                    ACT(ecs[:], ps[3][:], AF.Exp, [("ps", 3)], [("ecs", 0)])
                                    TT(v4(GT[:]), v4(dec[:]), cbm[:].unsqueeze(1).broadcast_to([128, 4, 128]), ALU.mult, [("dec",), ("cbm", 0)], [("GT", p)])
                                    TT(v4(Cs[:]), v4(ecs[:]), cmf[:].unsqueeze(1).broadcast_to([128, 4, 128]), ALU.mult, [("ecs", 0), ("cmf", 0)], [("Cs", p)])
                                    if c + 1 < 16:
                                        build_R(c + 1)

                                def stageB(c, g=g):
                                    p = c % 2
                                    col = c * 16 + 4 * g
                                    xdt, GT, Cs, xsf = xdt2[p], GT2[p], Cs2[p], xsf2[p]
                                    ACT(prevb[:], Hs[:], AF.Identity, [("Hs",)], [("prevb",)])
                                    v64 = lambda ap_: ap_.rearrange("p (h q) -> p h q", h=4)
                                    TT(v64(Hs[:]), v64(Hs[:]), cd[:, col:col + 4].unsqueeze(2).broadcast_to([128, 4, 64]), ALU.mult, [("Hs",), ("cd",)], [("Hs",)])
                                    TT(Hs[:], Hs[:], ps[4 + p][:, 0:256], ALU.add, [("Hs",), ("ps", 4 + p)], [("Hs",)])
                                    yb, ybk = prot.next()
                                    for h in range(4):
                                        hc = slice(h * 128, (h + 1) * 128)
                                        pc = slice((h // 2) * 128, (h // 2 + 1) * 128)
                                        MM(ps[yb][:, hc], xdt[:, pc], GT[:, hc], True, False, [("xdt", p * 4 + 2 * (h // 2)), ("xdt", p * 4 + 2 * (h // 2) + 1), ("GT", p)], [ybk])
                                        MM(ps[yb][:, hc], prevb[:, pc], Cs[:, hc], False, True, [("prevb",), ("Cs", p)], [ybk])
                                    c4 = c % 4
                                    for h in range(4):
                                        r0, p2 = (h % 2) * 64, h // 2
                                        rr = slice(r0, r0 + 64)
                                        STT(ysb[rr, p2, c4 * 128:(c4 + 1) * 128], xsf[rr, p2, :], dcol[rr, 2 * g + p2:2 * g + p2 + 1],
                                            ps[yb][rr, h * 128:(h + 1) * 128], ALU.mult, ALU.add, [("xsf", p), ("sm",), ybk], [("ysb", h)])
                                    if c4 == 3:
                                        tc = c // 4
                                        ts_ = slice(tc * 512, (tc + 1) * 512)
                                        for jj in range(2):
                                            wz, wzk = wload(l, f"z{2 * g + jj}")
                                            b, bk = prot.next()
                                            for k in range(8):
                                                MM(ps[b][:], wz[:, k * 128:(k + 1) * 128], hT[:, k, ts_], k == 0, k == 7, [wzk, ("hT", tc)], [bk])
                                            ACT(gz[:], ps[b][:], AF.Silu, [bk], [("sacc",)])
                                            TT(ysb[:, jj, :], ysb[:, jj, :], gz[:], ALU.mult, [("ysb",), ("sacc",)], [("ysb",)])
                                            ACT((GT2[1], Cs2[1])[jj][:], ysb[:, jj, :], AF.Square, [("ysb",)], [(("GT", 1), ("Cs", 1))[jj]])
                                        for jj in range(2):
                                            MM(ps[3][:], ones_b, (GT2[1], Cs2[1])[jj][:], jj == 0, jj == 1, [(("GT", 1), ("Cs", 1))[jj], ("cstb",)], [("ps", 3)])
                                        rstd_chain(grs[:, 0:512], ps[3][:], 1.0 / 256, [("ps", 3)], [("spre",)])
                                        for jj in range(2):
                                            STT(ysT[:, 2 * g + jj, ts_], ysb[:, jj, :], ssn[:, 2 * g + jj:2 * g + jj + 1], grs[:, 0:512], ALU.mult, ALU.mult,
                                                [("ysb",), ("spre",), ("sm",)], [("ysT",)], ss=("ysT",))

                                NCH = 16
                                stageA(0)
                                for c in range(NCH):
                                    if c + 1 < NCH:
                                        stageA(c + 1)
                                    stageB(c)
                            P.barrier()
                    P.barrier()
                    if l == 0:
                        tap("ysT", ysT[:, :, :], [128, 8, S], BF16, [("ysT",)])
                    merge_branch(l, ysT, ("ysT",), 8, "wc", 16, g_m)

            if "da" in branches:
                with ExitStack() as sk:
                    yaT = T("yaT", [128, 4, S], BF16, sk)
                    with ExitStack() as s2:
                        ps.append(s2.enter_context(nc.psum_tensor(f"ps7a{l}", [128, 512], F32)))
                        QN = T("QN", [128, S], BF16, s2)
                        wda = Rot("wda", [T(f"wda{i}", [128, 1024], BF16, s2) for i in range(4)])
                        KN = T("KN", [128, S], BF16, s2)
                        KN1 = T("KN1", [128, S], BF16, s2)
                        P.add("dve", lambda e, a_=KN[64:128, :]: e.memset(a_, 0.0), [], [("KN",)])
                        P.add("dve", lambda e, a_=KN1[0:64, :]: e.memset(a_, 0.0), [], [("KN1",)])
                        Vt = T("Vt", [128, S], BF16, s2)
                        Erot = Rot("E", [T(f"E{i}", [128, 512], BF16, s2) for i in range(3)])
                        tmpb = Rot("tmpb", [T(f"tmpb{i}", [128, 256], F32, s2) for i in range(2)])
                        rawR = Rot("raw", [T(f"raw{i}", [128, 512], F32, s2) for i in range(2)])
                        sqR = Rot("sq", [T(f"sq{i}", [128, 512], BF16, s2) for i in range(2)])
                        rsR = Rot("rs", [T(f"rs{i}", [128, 512], F32, s2) for i in range(2)])
                        ssR = Rot("ps", [3, 7])
                        on = [T(f"on{i}", [128, 512], F32, s2) for i in range(2)]
                        rinv = T("rinv", [128, 512], F32, s2)
                        oa = T("oa", [128, 512], F32, s2)
                        for h in range(4):
                            wq_ = {}

                            def qk_stage1(nm, dst, dk, wcol, tc, h=h, wq_=wq_):
                                if nm not in wq_:
                                    wq_[nm] = wload(l, f"{nm}{h}", wda)
                                wt, wk = wq_[nm]
                                ts_ = slice(tc * 512, (tc + 1) * 512)
                                b, bk = prot.next()
                                for k in range(8):
                                    MM(ps[b][:], wt[:, k * 128:(k + 1) * 128], hT[:, k, ts_], k == 0, k == 7, [wk, ("hT", tc)], [bk])
                                raw, rwk = rawR.next()
                                sq, sqk = sqR.next()
                                ACT(raw[:], ps[b][:], AF.Identity, [bk], [rwk])
                                ACT(sq[:], ps[b][:], AF.Square, [bk], [sqk])
                                return (raw, rwk, sq, sqk, dst, dk, wcol, ts_)

                            def qk_stage2(ctx):
                                raw, rwk, sq, sqk, dst, dk, wcol, ts_ = ctx
                                rs, rsk = rsR.next()
                                sb_, _ = ssR.next()
                                ssk = ("ps", sb_)
                                MM(ps[sb_][:], blk64_b, sq[:], True, True, [("cstb",), sqk], [ssk])
                                rstd_chain(rs[:], ps[sb_][:], 1.0 / 64, [ssk], [rsk])
                                if dk[0] == "KN":
                                    STT(KN[0:64, ts_], raw[0:64, :], wcol[0:64, 0:1], rs[0:64, :], ALU.mult, ALU.mult, [rwk, rsk, ("sm",)], [("KN",)], ss=("KN",))
                                    STT(KN1[64:128, ts_], raw[64:128, :], wcol[64:128, 0:1], rs[64:128, :], ALU.mult, ALU.mult, [rwk, rsk, ("sm",)], [("KN1",)], ss=("KN1",))
                                else:
                                    STT(dst[:, ts_], raw[:], wcol[:, 0:1], rs[:], ALU.mult, ALU.mult, [rwk, rsk, ("sm",)], [dk], ss=(dk[0],))

                            prev_ = None
                            for (nm, dst, dk, wcol) in (("qa", QN, ("QN",), smc("qn")), ("ka", KN, ("KN",), smc("kn"))):
                                for tc in range(NTC):
                                    cur_ = qk_stage1(nm, dst, dk, wcol, tc)
                                    if prev_ is not None:
                                        qk_stage2(prev_)
                                    prev_ = cur_
                            wv, wvk = wload(l, f"va{h}", wda)
                            for t4 in range(4):
                                b, bk = prot.next()
                                for i in range(4):
                                    tt = t4 * 4 + i
                                    for k in range(8):
                                        MM(ps[b][:, i * 128:(i + 1) * 128], hT[:, k, tt * 128:(tt + 1) * 128], wv[:, k * 128:(k + 1) * 128], k == 0, k == 7,
                                           [wvk, ("hT", t4)], [bk])
                                ACT(Vt[:, t4 * 512:(t4 + 1) * 512], ps[b][:], AF.Identity, [bk], [("Vt",)], ss=("Vt",))
                                if t4 == 0:
                                    qk_stage2(prev_)
                            dpend = [None]

                            dctx = {}

                            def da_finish(c, m, h=h):
                                bo, bs = ((4, 5), (6, 7))[m]
                                P.add("dve", lambda e, a_=rinv[:], b_=ps[bs][:]: e.reciprocal(a_, b_), [("ps", bs)], [("rinv",)])
                                TT(on[m][:], ps[bo][:], rinv[:], ALU.mult, [("ps", bo), ("rinv",)], [("on", m)])
                                if m == 1:
                                    STT(oa[:], on[1][:], neglam, on[0][:], ALU.mult, ALU.add, [("on", 0), ("on", 1), ("lsc",)], [("oa",)])
                                    sq, sqk = sqR.next()
                                    ACT(sq[:], oa[:], AF.Square, [("oa",)], [sqk])
                                    dctx["sq"] = (sq, sqk)

                            def da_finish2(c, m, h=h):
                                if m != 1:
                                    return
                                ts_ = slice(c * 512, (c + 1) * 512)
                                sq, sqk = dctx["sq"]
                                rs, rsk = rsR.next()
                                MM(ps[3][:], ones_b, sq[:], True, True, [("cstb",), sqk], [("ps", 3)])
                                rstd_chain(rs[:], ps[3][:], 1.0 / 128, [("ps", 3)], [rsk])
                                STT(yaT[:, h, ts_], oa[:], sublnw, rs[:], ALU.mult, ALU.mult, [("oa",), rsk, ("lsc",)], [("yaT",)], ss=("yaT",))

                            for c in range(NTC):
                                for m in range(2):
                                    bo, bs = ((4, 5), (6, 7))[m]
                                    attn_chunk(c, (KN, KN1)[m], QN, slice(0, 128), [(("KN",), ("KN1",))[m], ("QN",)], lambda j: Vt[:, j * 128:(j + 1) * 128], ("Vt",),
                                               band[:, h, :], 256, cbias[:, h:h + 1], 0.125, bo, bs, Erot, tmpb, hook=dpend[0])
                                    dpend[0] = [(2, (lambda c=c, m=m: da_finish(c, m))), (7, (lambda c=c, m=m: da_finish2(c, m)))]
                            for (_, fn_) in dpend[0]:
                                fn_()
                        P.barrier()
                        ps.pop()
                    if l == 0:
                        tap("yaT", yaT[:, :, :], [128, 4, S], BF16, [("yaT",)])
                    merge_branch(l, yaT, ("yaT",), 4, "wa", 0, g_m)

            if "mla" in branches:
                with ExitStack() as sk:
                    ybT = T("ybT", [128, 4, S], BF16, sk)
                    with ExitStack() as s2:
                        ps.append(s2.enter_context(nc.psum_tensor(f"ps7b{l}", [128, 512], F32)))
                        cqn = T("cqn", [128, 2, S], BF16, s2)
                        wml = Rot("wml", [T(f"wml{i}", [128, 1024], BF16, s2) for i in range(4)])
                        ckvn = T("ckvn", [128, S], BF16, s2)
                        krope = T("krope", [128, S], BF16, s2)
                        krsq = T("krsq", [128, S], BF16, s2)
                        Vb = T("Vb", [128, S], BF16, s2)
                        QH = T("QH", [128, S], BF16, s2)
                        KH = T("KH", [128, S], BF16, s2)
                        Erot = Rot("E", [T(f"Eb{i}", [128, 512], BF16, s2) for i in range(3)])
                        tmpb = Rot("tmpb", [T(f"tmpbm{i}", [128, 128], F32, s2) for i in range(2)])
                        sq2 = T("sq2", [128, 2, 512], BF16, s2)
                        sq4 = T("sq4", [128, 4, 512], BF16, s2)
                        rs = T("mrs", [128, 512], F32, s2)
                        rsB = T("mrsB", [128, 512], F32, s2)
                        t1 = T("t1", [128, 512], F32, s2)
                        t2 = T("t2", [128, 512], F32, s2)
                        cf = T("cf", [128, 512], F32, s2)
                        sf = T("sf", [128, 512], F32, s2)
                        raw2 = [t1, t2]
                        rinv = t2
                        qan, kvan, mqn, mkn = smc("qan"), smc("kvan"), smc("mqn"), smc("mkn")
                        R_ = slice(64, 96)
                        wcq = [wload(l, "cq0", wml), wload(l, "cq1", wml)]
                        for tc in range(NTC):
                            ts_ = slice(tc * 512, (tc + 1) * 512)
                            for j in range(2):
                                b, bk = prot.next()
                                for k in range(8):
                                    MM(ps[b][:], wcq[j][0][:, k * 128:(k + 1) * 128], hT[:, k, ts_], k == 0, k == 7, [wcq[j][1], ("hT", tc)], [bk])
                                ACT(raw2[j][:], ps[b][:], AF.Identity, [bk], [(("t1",), ("t2",))[j]])
                                ACT(sq2[:, j, :], ps[b][:], AF.Square, [bk], [("sq2", j)])
                            for j in range(2):
                                MM(ps[3][:], ones_b, sq2[:, j, :], j == 0, j == 1, [("cstb",), ("sq2", j)], [("ps", 3)])
                            rstd_chain(rs[:], ps[3][:], 1.0 / 256, [("ps", 3)], [("mrs",)])
                            for j in range(2):
                                STT(cqn[:, j, ts_], raw2[j][:], qan[:, j:j + 1], rs[:], ALU.mult, ALU.mult, [(("t1",), ("t2",))[j], ("mrs",), ("sm",)], [("cqn",)], ss=("cqn",))
                        wc_, wck = wload(l, "ckv", wml)
                        for tc in range(NTC):
                            ts_ = slice(tc * 512, (tc + 1) * 512)
                            b, bk = prot.next()
                            for k in range(8):
                                MM(ps[b][:], wc_[:, k * 128:(k + 1) * 128], hT[:, k, ts_], k == 0, k == 7, [wck, ("hT", tc)], [bk])
                            ACT(raw2[0][:], ps[b][:], AF.Identity, [bk], [("t1",)])
                            ACT(sq2[:, 0, :], ps[b][:], AF.Square, [bk], [("sq2", 0)])
                            MM(ps[3][:], ones_b, sq2[:, 0, :], True, True, [("cstb",), ("sq2", 0)], [("ps", 3)])
                            rstd_chain(rs[:], ps[3][:], 1.0 / 128, [("ps", 3)], [("mrs",)])
                            STT(ckvn[:, ts_], raw2[0][:], kvan[:, 0:1], rs[:], ALU.mult, ALU.mult, [("t1",), ("mrs",), ("sm",)], [("ckvn",)], ss=("ckvn",))
                        wkr, wkrk = wload(l, "kr", wml)
                        wkp, wkpk = wload(l, "krp", wml)
                        for tc in range(NTC):
                            ts_ = slice(tc * 512, (tc + 1) * 512)
                            b1, k1 = prot.next()
                            for k in range(8):
                                MM(ps[b1][0:96, :], wkr[:, k * 96:(k + 1) * 96], hT[:, k, ts_], k == 0, k == 7, [wkrk, ("hT", tc)], [k1])
                            b2, k2 = prot.next()
                            for k in range(8):
                                MM(ps[b2][0:96, :], wkp[:, k * 96:(k + 1) * 96], hT[:, k, ts_], k == 0, k == 7, [wkpk, ("hT", tc)], [k2])
                            ACT(krsq[R_, ts_], ps[b1][R_, :], AF.Square, [k1], [("krsq",)], ss=("krsq",))
                            ACT(cf[R_, :], cosT[R_, ts_], AF.Identity, [("cosT",)], [("cf",)])
                            ACT(sf[R_, :], sinS[R_, ts_], AF.Identity, [("sinS",)], [("sf",)])
                            STT(t1[R_, :], ps[b1][R_, :], mkn[R_, 0:1], cf[R_, :], ALU.mult, ALU.mult, [k1, ("sm",), ("cf",)], [("t1",)])
                            STT(t2[R_, :], ps[b2][R_, :], mkn[R_, 1:2], sf[R_, :], ALU.mult, ALU.mult, [k2, ("sm",), ("sf",)], [("t2",)])
                            TT(t1[R_, :], t1[R_, :], t2[R_, :], ALU.add, [("t1",), ("t2",)], [("t1",)])
                            ACT(krope[R_, ts_], t1[R_, :], AF.Identity, [("t1",)], [("krope",)], ss=("krope",))
                        for h in range(8):
                            if h % 2 == 0:
                                wv, wvk = wload(l, f"ukv{h // 2}", wml)
                                for t4 in range(4):
                                    b, bk = prot.next()
                                    for i in range(4):
                                        tt = t4 * 4 + i
                                        MM(ps[b][:, i * 128:(i + 1) * 128], ckvn[:, tt * 128:(tt + 1) * 128], wv[:, 0:128], True, True, [wvk, ("ckvn",)], [bk])
                                    ACT(Vb[:, t4 * 512:(t4 + 1) * 512], ps[b][:], AF.Identity, [bk], [("Vb",)], ss=("Vb",))
                            wq, wqk = wload(l, f"uq{h}", wml)
                            wqp, wqpk = wload(l, f"uqp{h}", wml)
                            wkn, wknk = wload(l, f"ukn{h}", wml)
                            mrot = Rot6()

                            def mla_stage1(tc, wq=wq, wqk=wqk, wqp=wqp, wqpk=wqpk, wkn=wkn, wknk=wknk):
                                ts_ = slice(tc * 512, (tc + 1) * 512)
                                par = tc % 2
                                b1, k1 = mrot.next()
                                for k in range(2):
                                    MM(ps[b1][0:96, :], wq[:, k * 96:(k + 1) * 96], cqn[:, k, ts_], k == 0, k == 1, [wqk, ("cqn",)], [k1])
                                b2, k2 = mrot.next()
                                for k in range(2):
                                    MM(ps[b2][0:96, :], wqp[:, k * 96:(k + 1) * 96], cqn[:, k, ts_], k == 0, k == 1, [wqpk, ("cqn",)], [k2])
                                b3, k3 = mrot.next()
                                MM(ps[b3][0:64, :], wkn[:, 0:64], ckvn[:, ts_], True, True, [wknk, ("ckvn",)], [k3])
                                ACT(sq4[0:96, par, :], ps[b1][0:96, :], AF.Square, [k1], [("sq4", par)])
                                ACT(sq4[0:64, 2 + par, :], ps[b3][0:64, :], AF.Square, [k3], [("sq4", 2 + par)])
                                return (ts_, par, b1, k1, b2, k2, b3, k3)

                            def mla_stage2(ctx):
                                ts_, par, b1, k1, b2, k2, b3, k3 = ctx
                                MM(ps[3][0:96, :], ones_b[0:96, 0:96], sq4[0:96, par, :], True, True, [("cstb",), ("sq4", par)], [("ps", 3)])
                                MM(ps[7][0:96, :], ones_b[0:64, 0:96], sq4[0:64, 2 + par, :], True, False, [("cstb",), ("sq4", 2 + par)], [("ps", 7)])
                                MM(ps[7][0:96, :], ones_b[R_, 0:96], krsq[R_, ts_], False, True, [("cstb",), ("krsq",)], [("ps", 7)])
                                rstd_chain(rs[0:96, :], ps[3][0:96, :], 1.0 / 96, [("ps", 3)], [("mrs",)])
                                rstd_chain(rsB[0:96, :], ps[7][0:96, :], 1.0 / 96, [("ps", 7)], [("mrsB",)])
                                STT(QH[0:64, ts_], ps[b1][0:64, :], mqn[0:64, 0:1], rs[0:64, :], ALU.mult, ALU.mult, [k1, ("mrs",), ("sm",)], [("QH",)], ss=("QH",))
                                ACT(cf[R_, :], cosT[R_, ts_], AF.Identity, [("cosT",)], [("cf",)])
                                ACT(sf[R_, :], sinS[R_, ts_], AF.Identity, [("sinS",)], [("sf",)])
                                STT(t1[R_, :], ps[b1][R_, :], mqn[R_, 0:1], cf[R_, :], ALU.mult, ALU.mult, [k1, ("sm",), ("cf",)], [("t1",)])
                                STT(t2[R_, :], ps[b2][R_, :], mqn[R_, 1:2], sf[R_, :], ALU.mult, ALU.mult, [k2, ("sm",), ("sf",)], [("t2",)])
                                TT(t1[R_, :], t1[R_, :], t2[R_, :], ALU.add, [("t1",), ("t2",)], [("t1",)])
                                TT(QH[R_, ts_], t1[R_, :], rs[R_, :], ALU.mult, [("t1",), ("mrs",)], [("QH",)], ss=("QH",))
                                STT(KH[0:64, ts_], ps[b3][0:64, :], mkn[0:64, 0:1], rsB[0:64, :], ALU.mult, ALU.mult, [k3, ("mrsB",), ("sm",)], [("KH",)], ss=("KH",))
                                ACT(sf[R_, :], krope[R_, ts_], AF.Identity, [("krope",)], [("sf",)])
                                TT(KH[R_, ts_], sf[R_, :], rsB[R_, :], ALU.mult, [("sf",), ("mrsB",)], [("KH",)], ss=("KH",))

                            prev_ = None
                            for tc in range(NTC):
                                cur_ = mla_stage1(tc)
                                if prev_ is not None:
                                    mla_stage2(prev_)
                                prev_ = cur_
                            mla_stage2(prev_)
                            mpend = [None]

                            def mla_finish(c, h=h):
                                ts_ = slice(c * 512, (c + 1) * 512)
                                bo, bs = ((4, 5), (6, 7))[c % 2]
                                rr = slice((h % 2) * 64, (h % 2) * 64 + 64)
                                P.add("dve", lambda e, a_=rinv[rr, :], b_=ps[bs][rr, :]: e.reciprocal(a_, b_), [("ps", bs)], [("t2",)])
                                TT(ybT[rr, h // 2, ts_], ps[bo][rr, :], rinv[rr, :], ALU.mult, [("ps", bo), ("t2",)], [("ybT",)], ss=("ybT",))

                            for c in range(NTC):
                                bo, bs = ((4, 5), (6, 7))[c % 2]
                                attn_chunk(c, KH, QH, slice(0, 96), [("KH",), ("QH",)], lambda j: Vb[:, j * 128:(j + 1) * 128], ("Vb",),
                                           maskb, 128, 0.0, 96 ** -0.5, bo, bs, Erot, tmpb, hook=mpend[0])
                                mpend[0] = (lambda c=c: mla_finish(c))
                            mpend[0]()
                        P.barrier()
                        ps.pop()
                    if l == 0:
                        tap("ybT", ybT[:, :, :], [128, 4, S], BF16, [("ybT",)])
                    merge_branch(l, ybT, ("ybT",), 4, "wb", 8, g_m)
            if l == 0:
                tap("x_mid", x[:, :, :], [128, 8, S], F32, [("x",)])

        for l in range(nlayers):
            lam_init = 0.8 - 0.6 * math.exp(-0.3 * l)
            DMA("sp", sm[:], sm_d[l], [], [("sm",)])
            mod = modt[:, l * 48:(l + 1) * 48]
            sh_m, sc_m, g_m = mod[:, 0:8], mod[:, 8:16], mod[:, 16:24]
            sh_f, sc_f, g_f = mod[:, 24:32], mod[:, 32:40], mod[:, 40:48]
            G1, G2 = lsc[:, 0:8], lsc[:, 8:16]
            neglam, sublnw = lsc[:, 16:17], lsc[:, 17:18]
            STT(G1, sc_m, 1.0, smc("norm_mix"), ALU.add, ALU.mult, [("modt",), ("sm",)], [("lsc",)])
            STT(G2, sc_f, 1.0, smc("norm_ffn"), ALU.add, ALU.mult, [("modt",), ("sm",)], [("lsc",)])
            lp = smc("lam")
            lt = lsc[:, 24:28]
            with ExitStack() as sk:
                lpp = T("lpp", [128, 128], F32, sk)
                TT(lpp[:, 0:64], lp[:, 0:64], lp[:, 64:128], ALU.mult, [("sm",)], [("lpp",)])
                TT(lpp[:, 64:128], lp[:, 128:192], lp[:, 192:256], ALU.mult, [("sm",)], [("lpp",)])
                P.add("dve", lambda e, lpp=lpp, lt=lt: e.reduce_sum(lt[:, 0:1], lpp[:, 0:64], mybir.AxisListType.X), [("lpp",)], [("lsc",)])
                P.add("dve", lambda e, lpp=lpp, lt=lt: e.reduce_sum(lt[:, 1:2], lpp[:, 64:128], mybir.AxisListType.X), [("lpp",)], [("lsc",)])
                ACT(lt[:, 0:2], lt[:, 0:2], AF.Exp, [("lsc",)], [("lsc",)])
                TT(lt[:, 2:3], lt[:, 1:2], lt[:, 0:1], ALU.subtract, [("lsc",)], [("lsc",)])
                TS(neglam, lt[:, 2:3], -lam_init, None, ALU.add, None, [("lsc",)], [("lsc",)])
                TS(sublnw, smc("subln"), 1.0 - lam_init, None, ALU.mult, None, [("sm",)], [("lsc",)])
                norm_to_hT(G1, sh_m, sk)
            P.barrier()
            if l == 0:
                tap("hT", hT[:, :, :], [128, 8, S], BF16, [("hT",)])

            if do_mixer:
                mixer(l, g_m, neglam, sublnw)
                if P.muted:
                    P.muted = False
                    P.barrier()

            if do_ffn:
                with ExitStack() as sk:
                    norm_to_hT(G2, sh_f, sk)
                P.barrier()
                if l == 0:
                    tap("h2T", hT[:, :, :], [128, 8, S], BF16, [("hT",)])
                with ExitStack() as sk:
                    gbuf = T("gbuf", [128, NF, 1024], BF16, sk)
                    pre = [[T(f"fpre{w_}{p_}", [128, 514], F32, sk) for p_ in range(2)] for w_ in range(2)]
                    acc = [[T(f"facc{w_}{p_}", [128, 512], F32, sk) for p_ in range(2)] for w_ in range(2)]
                    halo = T("halo", [128, 44, 2], F32, sk)
                    hl = T("hl", [128, 2, 2], F32, sk)
                    wbig = [T(f"wbig{i}", [128, NF * 128], BF16, sk) for i in range(2)]
                    bigrot = Rot("wbig", wbig)
                    wffn = Rot("wf", [T(f"wf{i}", [128, 1024], BF16, sk) for i in range(6)])
                    fcw, fcb = smc("fcw"), smc("fcb")
                    fit = [0]

                    def ffn_stage1(half, i, t2, wts):
                        tc = half * 2 + t2
                        ts_ = slice(tc * 512, (tc + 1) * 512)
                        par = fit[0] % 2
                        fit[0] += 1
                        for which in range(2):
                            wt, wk = wts[which]
                            j = which * NF + i
                            b, bk = prot6.next()
                            for k in range(8):
                                MM(ps[b][:], wt[:, k * 128:(k + 1) * 128], hT[:, k, ts_], k == 0, k == 7, [wk, ("hT", tc)], [bk])
                            conv_A(ps[b][:], pre[which][par], acc[which][par][:], 3, [fcw[:, j * 3 + k:j * 3 + k + 1] for k in range(3)],
                                   fcb[:, j:j + 1], ("fpre", which * 2 + par), ("facc", which * 2 + par), [bk])
                        return (half, i, t2, par)

                    def ffn_stage2(ctx):
                        half, i, t2, par = ctx
                        tc = half * 2 + t2
                        for which in range(2):
                            j = which * NF + i
                            pk, ak = ("fpre", which * 2 + par), ("facc", which * 2 + par)
                            hk = ("hl", which)
                            if tc == 2:
                                CP(hl[:, which, :], halo[:, j, :], [("halo", j)], [hk])
                            conv_B(pre[which][par], acc[which][par][:], 3, [fcw[:, j * 3 + k:j * 3 + k + 1] for k in range(3)], tc == 0, pk, ak,
                                   hl[:, which, :], hk)
                            if tc == 1:
                                CP(halo[:, j, :], hl[:, which, :], [hk], [("halo", j)])
                        a0, a1 = acc[0][par], acc[1][par]
                        ACT(a0[:], a0[:], AF.Silu, [("facc", par)], [("facc", par)])
                        TT(gbuf[:, i, t2 * 512:(t2 + 1) * 512], a0[:], a1[:], ALU.mult, [("facc", par), ("facc", 2 + par)], [("gbuf", i * 2 + t2)])

                    for half in range(2):
                        prev_ = None
                        for i in range(NF):
                            wts = [wload(l, f"ua{i}", wffn), wload(l, f"uv{i}", wffn)]
                            for t2 in range(2):
                                cur_ = ffn_stage1(half, i, t2, wts)
                                if prev_ is not None:
                                    ffn_stage2(prev_)
                                prev_ = cur_
                        ffn_stage2(prev_)
                        for io in range(8):
                            wd, wdk = wload(l, f"dn{io}", bigrot)
                            for t2 in range(2):
                                tc = half * 2 + t2
                                ts_ = slice(tc * 512, (tc + 1) * 512)
                                b, bk = prot6.next()
                                for k in range(NF):
                                    MM(ps[b][:], wd[:, k * 128:(k + 1) * 128], gbuf[:, k, t2 * 512:(t2 + 1) * 512], k == 0, k == NF - 1,
                                       [wdk, ("gbuf", k * 2 + t2)], [bk])
                                STT(x[:, io, ts_], ps[b][:], g_f[:, io:io + 1], x[:, io, ts_], ALU.mult, ALU.add, [bk, ("x", tc), ("modt",)], [("x", tc)], ss=("x",))
                P.barrier()
            if l == 0:
                tap("x_l0", x[:, :, :], [128, 8, S], F32, [("x",)])

        outs = []
        for c in range(8):
            outs.append(DMA("sp", out_d[c * 128:(c + 1) * 128, :], x[:, c, :], [("x",)], [("out", c)]))
        P.add("sp", None, [("out", c) for c in range(8)] + [("tapout", n) for n in tap_d], [])
        info = P.emit(nc, st)
    return nc, info, list(tap_d.keys())


def _host_inputs(inputs, cores):
    inp = {k: np.asarray(v) for k, v in inputs.items()}
    wblob = np.stack([_weight_blob(inp, l) for l in range(NL)])
    smb = np.stack([_small_blob(inp, l).build() for l in range(NL)])
    cst = _const_blob().build()
    relb = np.ascontiguousarray(inp["rel_bias"], dtype=np.float32)
    maps = []
    for b in cores:
        maps.append({
            "xT": np.ascontiguousarray(inp["x"][b].T, dtype=np.float32),
            "cst": cst, "sm": smb,
            "cvec": _col(inp["c"][b]),
            "pos": np.ascontiguousarray(np.broadcast_to(inp["positions"][b][None, :], (32, S)), dtype=np.int32),
            "relb": relb, "wblob": wblob,
        })
    return maps


def kernel(**inputs):
    nc, info, _ = build()
    maps = _host_inputs(inputs, range(8))
    res = run_bass_kernel_spmd(nc, maps, core_ids=list(range(8)))
    return np.stack([np.ascontiguousarray(res.results[b]["outT"].T) for b in range(8)]).astype(np.float32)
```

```python
import math
from contextlib import ExitStack
import numpy as np
import concourse.bass as bass
import concourse.mybir as mybir
from concourse.bass_utils import run_bass_kernel_spmd

F32 = mybir.dt.float32
BF16 = mybir.dt.bfloat16
I32 = mybir.dt.int32
AF = mybir.ActivationFunctionType
ALU = mybir.AluOpType

D = 1024
S = 2048
NL = 4
NTC = 4
NTT = 16
FFN = 2816
NF = 22
IN_SIZES = (512, 512, 512, 256, 128, 32, 1024, 2048, 16, 3072)
IN_OFF = [0] + [int(v) for v in np.cumsum(IN_SIZES)]
O_QA, O_KA, O_VA, O_CQ, O_CKV, O_KR, O_Z, O_XBC, O_DT, O_G = IN_OFF[:10]
EPS = 1e-6
NEG = -30000.0


class _Op:
    __slots__ = ("i", "eng", "fn", "deps", "raw", "nss", "skip", "signal", "sidx", "dma", "dsem", "dval", "dprev")

    def __init__(self, i, eng, fn, dma):
        self.i, self.eng, self.fn, self.dma = i, eng, fn, dma
        self.deps = set()
        self.raw = set()
        self.nss = False
        self.skip = set()
        self.signal = False
        self.sidx = -1
        self.dsem = self.dval = -1
        self.dprev = None


class Prog:
    ENGS = ("pe", "act", "dve", "pool", "sp")
    EPOCH = 512
    NSEM = 8
    NDMA = 32

    def __init__(self):
        self.ops = []
        self.st = {}
        self.last = {}
        self._dmas = []
        self.muted = False

    def _node(self, name):
        n = self.st.get(name)
        if n is None:
            n = self.st[name] = {"w": [None, []], "p": {}}
        return n

    @staticmethod
    def _addreader(lst, op):
        if not op.dma:
            for k, o in enumerate(lst):
                if (not o.dma) and o.eng == op.eng:
                    lst[k] = op
                    return
        lst.append(op)

    def _collect(self, reads, writes):
        deps = set()
        for key in reads:
            n = self._node(key[0])
            if n["w"][0] is not None:
                deps.add(n["w"][0])
            if len(key) == 1:
                for p in n["p"].values():
                    if p[0] is not None:
                        deps.add(p[0])
            else:
                p = n["p"].get(key[1])
                if p is not None and p[0] is not None:
                    deps.add(p[0])
        for key in writes:
            n = self._node(key[0])
            if n["w"][0] is not None:
                deps.add(n["w"][0])
            deps.update(n["w"][1])
            if len(key) == 1:
                for p in n["p"].values():
                    if p[0] is not None:
                        deps.add(p[0])
                    deps.update(p[1])
            else:
                p = n["p"].get(key[1])
                if p is not None:
                    if p[0] is not None:
                        deps.add(p[0])
                    deps.update(p[1])
        return deps

    def add(self, eng, fn, reads=(), writes=(), dma=False, ss=True):
        op = _Op(len(self.ops), eng, fn, dma)
        if self.muted:
            return op
        if ss is not True:
            relaxed = set(ss)
            strict = self._collect([k for k in reads if k[0] not in relaxed], [k for k in writes if k[0] not in relaxed])
            loose = self._collect([k for k in reads if k[0] in relaxed], [k for k in writes if k[0] in relaxed])
            op.skip = {d for d in (loose - strict) if d.eng == eng and not d.dma}
        deps = op.deps
        for key in reads:
            n = self._node(key[0])
            if n["w"][0] is not None:
                deps.add(n["w"][0])
            if len(key) == 1:
                for p in n["p"].values():
                    if p[0] is not None:
                        deps.add(p[0])
            else:
                p = n["p"].get(key[1])
                if p is not None and p[0] is not None:
                    deps.add(p[0])
        op.raw = set(deps)
        for key in writes:
            n = self._node(key[0])
            if n["w"][0] is not None:
                deps.add(n["w"][0])
            deps.update(n["w"][1])
            if len(key) == 1:
                for p in n["p"].values():
                    if p[0] is not None:
                        deps.add(p[0])
                    deps.update(p[1])
            else:
                p = n["p"].get(key[1])
                if p is not None:
                    if p[0] is not None:
                        deps.add(p[0])
                    deps.update(p[1])
        for key in reads:
            n = self._node(key[0])
            if len(key) == 1:
                self._addreader(n["w"][1], op)
            else:
                p = n["p"].setdefault(key[1], [None, []])
                self._addreader(p[1], op)
        for key in writes:
            n = self._node(key[0])
            if len(key) == 1:
                n["w"] = [op, []]
                n["p"] = {}
            else:
                n["p"][key[1]] = [op, []]
        deps.discard(op)
        self.ops.append(op)
        self.last[eng] = op
        if dma:
            self._dmas.append(op)
        return op

    def barrier(self):
        lasts = list(self.last.values())
        dmas = self._dmas
        self._dmas = []
        for eng in self.ENGS:
            op = self.add(eng, None)
            op.deps.update(lasts)
            op.deps.update(dmas)
            op.deps.discard(op)
            op.raw = set(op.deps)

    def emit(self, nc, stack):
        ops = self.ops
        for op in ops:
            for d in op.deps:
                if (not d.dma) and (d.eng != op.eng or (op.eng != "pe" and d not in op.skip)):
                    d.signal = True
        cnt = {e: 0 for e in self.ENGS}
        for op in ops:
            if op.signal:
                op.sidx = cnt[op.eng]
                cnt[op.eng] += 1
        sems = {e: [stack.enter_context(nc.semaphore(f"s_{e}{k}")) for k in range(self.NSEM)] for e in self.ENGS}
        dsems = [stack.enter_context(nc.semaphore(f"d_{k}")) for k in range(self.NDMA)]
        uses = [0] * self.NDMA
        lastd = [None] * self.NDMA
        half = self.NDMA // 2
        nxt = {"pool": 0, "sp": 0}
        for op in ops:
            if op.dma:
                q = "pool" if op.eng == "pool" else "sp"
                k = nxt[q] % half + (0 if q == "pool" else half)
                nxt[q] += 1
                op.dsem = k
                uses[k] += 1
                op.dval = 16 * uses[k]
                op.dprev = lastd[k]
                lastd[k] = op

        def sig(op):
            ep, r = divmod(op.sidx, self.EPOCH)
            return sems[op.eng][ep % self.NSEM], (ep // self.NSEM) * self.EPOCH + r + 1

        by_eng = {e: [o for o in ops if o.eng == e] for e in self.ENGS}
        nwaits = [0]

        def run(eng_name, e):
            waited_eng = {}
            waited_dma = {}
            for op in by_eng[eng_name]:
                need_eng = {}
                for d in op.deps:
                    if d.dma:
                        if waited_dma.get(d.dsem, 0) < d.dval:
                            waited_dma[d.dsem] = d.dval
                            e.wait_ge(dsems[d.dsem], d.dval)
                            nwaits[0] += 1
                    elif d.eng != eng_name or (eng_name != "pe" and d not in op.skip):
                        if d.sidx > need_eng.get(d.eng, -1):
                            need_eng[d.eng] = d.sidx
                if op.dma and op.dprev is not None:
                    d = op.dprev
                    if waited_dma.get(d.dsem, 0) < d.dval:
                        waited_dma[d.dsem] = d.dval
                        e.wait_ge(dsems[d.dsem], d.dval)
                for pe_, sidx in need_eng.items():
                    if sidx > waited_eng.get(pe_, -1):
                        waited_eng[pe_] = sidx
                        ep, r = divmod(sidx, self.EPOCH)
                        e.wait_ge(sems[pe_][ep % self.NSEM], (ep // self.NSEM) * self.EPOCH + r + 1)
                        nwaits[0] += 1
                if op.fn is None:
                    if op.signal:
                        s_, v_ = sig(op)
                        e.nop().then_inc(s_, 1)
                    continue
                ins = op.fn(e)
                if op.dma:
                    ins.then_inc(dsems[op.dsem], 16)
                elif op.signal:
                    s_, v_ = sig(op)
                    ins.then_inc(s_, 1)

        with nc.Block() as block:
            @block.tensor
            def _(e):
                run("pe", e)

            @block.scalar
            def _(e):
                run("act", e)

            @block.vector
            def _(e):
                run("dve", e)

            @block.gpsimd
            def _(e):
                run("pool", e)

            @block.sync
            def _(e):
                run("sp", e)
        return dict(n_ops=len(ops), n_waits=nwaits[0], signals=dict(cnt))


def _tile_w(W, cols):
    K = W.shape[0] // 128
    sub = W[:, cols]
    return np.ascontiguousarray(sub.reshape(K, 128, len(cols)).transpose(1, 0, 2).reshape(128, K * len(cols)))


class _Blob:
    def __init__(self):
        self.parts, self.off, self.n = [], {}, 0

    def add(self, name, arr):
        arr = np.asarray(arr, dtype=np.float32)
        assert arr.shape[0] == 128
        self.off[name] = (self.n, arr.shape[1])
        self.parts.append(arr)
        self.n += arr.shape[1]

    def build(self):
        return np.ascontiguousarray(np.concatenate(self.parts, axis=1))


def _ar(a, b):
    return np.arange(a, b)


def _weight_specs():
    sp = []
    for j in range(48):
        sp.append((f"ada{j}", "ada_w", _ar(j * 128, (j + 1) * 128)))
    for h in range(4):
        sp.append((f"qa{h}", "w_in", _ar(O_QA + h * 128, O_QA + (h + 1) * 128)))
        sp.append((f"ka{h}", "w_in", _ar(O_KA + h * 128, O_KA + (h + 1) * 128)))
        sp.append((f"va{h}", "w_in", _ar(O_VA + h * 128, O_VA + (h + 1) * 128)))
    for j in range(2):
        sp.append((f"cq{j}", "w_in", _ar(O_CQ + j * 128, O_CQ + (j + 1) * 128)))
    sp.append(("ckv", "w_in", _ar(O_CKV, O_CKV + 128)))
    sp.append(("kr", "w_in", _ar(O_KR - 64, O_KR + 32)))
    sp.append(("krp", "w_in", np.concatenate([_ar(O_KR - 64, O_KR), _ar(O_KR + 16, O_KR + 32), _ar(O_KR, O_KR + 16)])))
    for j in range(8):
        sp.append((f"z{j}", "w_in", _ar(O_Z + j * 128, O_Z + (j + 1) * 128)))
    for j in range(16):
        sp.append((f"xbc{j}", "w_in", _ar(O_XBC + j * 128, O_XBC + (j + 1) * 128)))
    sp.append(("dt", "w_in", _ar(O_DT, O_DT + 16)))
    for j in range(24):
        sp.append((f"g{j}", "w_in", _ar(O_G + j * 128, O_G + (j + 1) * 128)))
    for h in range(8):
        sp.append((f"uq{h}", "mla_w_uq", _ar(h * 96, h * 96 + 96)))
        sp.append((f"uqp{h}", "mla_w_uq", np.concatenate([_ar(h * 96, h * 96 + 64), _ar(h * 96 + 80, h * 96 + 96), _ar(h * 96 + 64, h * 96 + 80)])))
        sp.append((f"ukn{h}", "mla_w_ukv", _ar(h * 128, h * 128 + 64)))
    for pr in range(4):
        sp.append((f"ukv{pr}", "mla_w_ukv", np.concatenate([_ar(2 * pr * 128 + 64, 2 * pr * 128 + 128), _ar((2 * pr + 1) * 128 + 64, (2 * pr + 1) * 128 + 128)])))
    for i in range(8):
        sp.append((f"wa{i}", "w_branch_a", _ar(i * 128, (i + 1) * 128)))
        sp.append((f"wb{i}", "w_branch_b", _ar(i * 128, (i + 1) * 128)))
        sp.append((f"wc{i}", "w_branch_c", _ar(i * 128, (i + 1) * 128)))
        sp.append((f"wo{i}", "w_out", _ar(i * 128, (i + 1) * 128)))
        sp.append((f"dn{i}", "ffn_w_down", _ar(i * 128, (i + 1) * 128)))
    for i in range(NF):
        sp.append((f"ua{i}", "ffn_w_up", _ar(i * 128, (i + 1) * 128)))
        sp.append((f"uv{i}", "ffn_w_up", _ar(FFN + i * 128, FFN + (i + 1) * 128)))
    return sp


_WROWS = {"ada_w": 1024, "w_in": 1024, "mla_w_uq": 256, "mla_w_ukv": 128, "w_branch_a": 512, "w_branch_b": 512,
          "w_branch_c": 1024, "w_out": 1024, "ffn_w_down": FFN, "ffn_w_up": 1024}


def _weight_offsets():
    off, n = {}, 0
    for name, src, cols in _weight_specs():
        w = (_WROWS[src] // 128) * len(cols)
        off[name] = (n, w)
        n += w
    return off, n


def _weight_blob(inp, l):
    return np.ascontiguousarray(np.concatenate([_tile_w(inp[src][l], cols) for _, src, cols in _weight_specs()], axis=1))


def _col(v, n=128):
    v = np.asarray(v, np.float32)
    return np.ascontiguousarray(v.reshape(-1, n).T)


def _small_blob(inp, l):
    b = _Blob()
    b.add("ada_b", _col(inp["ada_b"][l]))
    b.add("norm_mix", _col(inp["norm_mix"][l]))
    b.add("norm_ffn", _col(inp["norm_ffn"][l]))
    b.add("qn", np.tile(inp["da_q_norm"][l], 2)[:, None])
    b.add("kn", np.tile(inp["da_k_norm"][l], 2)[:, None])
    b.add("subln", inp["da_subln"][l][:, None])
    b.add("lam", np.broadcast_to(inp["da_lambda"][l].reshape(1, 256), (128, 256)))
    b.add("qan", _col(inp["mla_q_a_norm"][l]))
    b.add("kvan", _col(inp["mla_kv_a_norm"][l]))
    qn = np.zeros((128, 2), np.float32)
    w = inp["mla_q_norm"][l]
    qn[:96, 0] = w
    qn[64:96, 1] = np.concatenate([w[80:96], w[64:80]])
    b.add("mqn", qn)
    kn = np.zeros((128, 2), np.float32)
    w = inp["mla_k_norm"][l]
    kn[:96, 0] = w
    kn[64:96, 1] = np.concatenate([w[80:96], w[64:80]])
    b.add("mkn", kn)
    cw = inp["ssm_conv_w"][l]
    b.add("scw", np.concatenate([_col(cw[k]) for k in range(4)], axis=1).reshape(128, 4, 16).transpose(0, 2, 1).reshape(128, 64))
    b.add("scb", _col(inp["ssm_conv_b"][l]))
    b.add("dtb", np.broadcast_to(np.tile(inp["ssm_dt_bias"][l], 16)[None, :], (128, 256)))
    b.add("alog", np.broadcast_to(np.tile(inp["ssm_a_log"][l], 16)[None, :], (128, 256)))
    b.add("ssd_d", np.repeat(inp["ssm_d"][l], 64).reshape(8, 128).T)
    b.add("ssn", _col(inp["ssm_norm"][l]))
    fw = inp["ffn_conv_w"][l]
    b.add("fcw", np.concatenate([_col(fw[k]) for k in range(3)], axis=1).reshape(128, 3, 44).transpose(0, 2, 1).reshape(128, 132))
    b.add("fcb", _col(inp["ffn_conv_b"][l]))
    return b


def _t5_bucket_np(d):
    d = np.maximum(d, 0)
    large = 16 + (np.log(np.maximum(d, 1).astype(np.float32) / 16) / math.log(128 / 16) * 16).astype(np.int32)
    large = np.minimum(large, 31)
    return np.where(d < 16, d, large)


def _const_blob():
    b = _Blob()
    b.add("ident", np.eye(128, dtype=np.float32))
    b.add("ones", np.ones((128, 128), np.float32))
    blk = np.zeros((128, 128), np.float32)
    blk[:64, :64] = 1
    blk[64:, 64:] = 1
    b.add("blk64", blk)
    b.add("tri", np.triu(np.ones((128, 128), np.float32)))
    b.add("utri", np.tril(np.ones((128, 128), np.float32), -1))
    sel = np.zeros((128, 512), np.float32)
    for h in range(4):
        sel[h, h * 128:(h + 1) * 128] = 1
    b.add("sel4", sel)
    kk = np.arange(128)[:, None]
    u = np.arange(128)[None, :]
    b.add("maskb", np.where(u >= kk, 0.0, NEG).astype(np.float32))
    oh = np.zeros((128, 384), np.float32)
    m = np.arange(384)
    d = m - 127
    bk = _t5_bucket_np(d)
    for mm in range(384):
        if d[mm] < 0:
            oh[32, mm] = 1
        else:
            oh[bk[mm], mm] = 1
    b.add("oh", oh)
    invf = np.zeros((128, 2), np.float32)
    f = (10000.0 ** (-np.arange(0, 32, 2, dtype=np.float32) / 32)).astype(np.float32)
    invf[64:96, 0] = np.concatenate([f, f])
    invf[64:80, 1] = -1
    invf[80:96, 1] = 1
    b.add("invf", invf)
    return b


_SHAPES = {"ada_b": (NL, 6144), "norm_mix": (NL, D), "norm_ffn": (NL, D), "da_q_norm": (NL, 64), "da_k_norm": (NL, 64),
           "da_subln": (NL, 128), "da_lambda": (NL, 4, 64), "mla_q_a_norm": (NL, 256), "mla_kv_a_norm": (NL, 128),
           "mla_q_norm": (NL, 96), "mla_k_norm": (NL, 96), "ssm_conv_w": (NL, 4, 2048), "ssm_conv_b": (NL, 2048),
           "ssm_dt_bias": (NL, 16), "ssm_a_log": (NL, 16), "ssm_d": (NL, 16), "ssm_norm": (NL, D),
           "ffn_conv_w": (NL, 3, 2 * FFN), "ffn_conv_b": (NL, 2 * FFN)}


class _StopBuild(Exception):
    pass


def build(nlayers=NL, taps=(), do_mixer=True, do_ffn=True, branches=("ssd", "da", "mla"), dbg=None):
    dbg = dbg or {}
    taps = set(taps)
    nc = bass.Bass("TRN2", target_bir_lowering=False)
    P = Prog()
    woff, WTOT = _weight_offsets()
    dummy = {k: np.zeros(v, np.float32) for k, v in _SHAPES.items()}
    sb_ = _small_blob(dummy, 0)
    smoff, SMTOT = sb_.off, sb_.n
    cb_ = _const_blob()
    coff, CTOT = cb_.off, cb_.n

    def dr(name, shape, dt, kind):
        return nc.dram_tensor(name, shape, dt, kind=kind).ap()

    xT_d = dr("xT", [D, S], F32, "ExternalInput")
    cst_d = dr("cst", [128, CTOT], F32, "ExternalInput")
    sm_d = dr("sm", [NL, 128, SMTOT], F32, "ExternalInput")
    cv_d = dr("cvec", [128, 8], F32, "ExternalInput")
    pos_d = dr("pos", [32, S], I32, "ExternalInput")
    relb_d = dr("relb", [32, 4], F32, "ExternalInput")
    wb_d = dr("wblob", [NL, 128, WTOT], F32, "ExternalInput")
    out_d = dr("outT", [D, S], F32, "ExternalOutput")
    zscr = dr("zscr", [4, 128, 384], F32, "Internal")
    yscr = dr("yscr", [8, 128, S], BF16, "Internal")
    tap_d = {}

    with ExitStack() as st:
        tcount = [0]

        def T(name, shape, dt, stk=None):
            tcount[0] += 1
            return (stk or st).enter_context(nc.sbuf_tensor(f"t{tcount[0]}_{name}", shape, dt))

        ps = [st.enter_context(nc.psum_tensor(f"ps{i}", [128, 512], F32)) for i in range(7)]

        def MM(out, lhsT, rhs, start, stop, r, w):
            P.add("pe", lambda e: e.matmul(out, lhsT, rhs, start=start, stop=stop), r, w)

        def ACT(out, in_, func, r, w, bias=None, scale=None, ss=True):
            kw = {}
            if bias is not None:
                kw["bias"] = bias
            if scale is not None:
                kw["scale"] = scale
            P.add("act", lambda e: e.activation(out, in_, func, **kw), r, w, ss=ss)

        def TT(out, in0, in1, op, r, w, eng="dve", ss=True):
            P.add(eng, lambda e: e.tensor_tensor(out, in0, in1, op), r, w, ss=ss)

        def TS(out, in0, s1, s2, op0, op1, r, w, eng="dve"):
            if s2 is None:
                P.add(eng, lambda e: e.tensor_scalar(out, in0, s1, None, op0), r, w)
            else:
                P.add(eng, lambda e: e.tensor_scalar(out, in0, s1, s2, op0, op1), r, w)

        def STT(out, in0, sc, in1, op0, op1, r, w, eng="dve", ss=True):
            P.add(eng, lambda e: e.scalar_tensor_tensor(out, in0, sc, in1, op0, op1), r, w, ss=ss)

        def CP(out, in_, r, w, eng="dve"):
            P.add(eng, lambda e: e.tensor_copy(out, in_), r, w)

        def DMA(q, out, in_, r, w):
            return P.add(q, lambda e: e.dma_start(out=out, in_=in_), r, w, dma=True)

        class Rot:
            def __init__(self, name, items):
                self.name, self.items, self.i = name, items, 0

            def next(self):
                k = self.i % len(self.items)
                self.i += 1
                return self.items[k], (self.name, k)

        def tap(name, src_ap, shape, dt, keys):
            if name not in taps:
                return
            d = dr("tap_" + name, shape, dt, "ExternalOutput")
            tap_d[name] = d
            DMA("sp", d, src_ap, keys, [("tapout", name)])

        x = T("x_sb", [128, 8, S], F32)
        hT = T("hT", [128, 8, S], BF16)
        cstf = T("cstf", [128, 640], F32)
        ones_f = cstf[:, 0:128]
        tri_f = cstf[:, 128:256]
        utri_f = cstf[:, 256:384]
        maskb = cstf[:, 384:512]
        ident_f = cstf[:, 512:640]
        cstb = T("cstb", [128, 640], BF16)
        ident_b = cstb[:, 0:128]
        ones_b = cstb[:, 128:256]
        blk64_b = cstb[:, 256:384]
        tri_b = cstb[:, 384:512]
        utri_b = cstb[:, 512:640]
        sm = T("sm_sb", [128, SMTOT], F32)
        cosT = T("cosT", [128, S], BF16)
        sinS = T("sinS", [128, S], BF16)
        band = T("band", [128, 4, 256], F32)
        cbias = T("cbias", [128, 4], F32)
        modt = T("modt", [128, NL * 48], F32)
        lsc = T("lsc", [128, 64], F32)
        wsl = [T(f"wsl{i}", [128, 1024], BF16) for i in range(3)]
        wrot = Rot("w", wsl)

        def smc(name, a=0, b=None):
            o, n = smoff[name]
            return sm[:, o + a:o + (n if b is None else b)]

        def wload(l, name, rot=None):
            o, n = woff[name]
            (buf, key) = (rot or wrot).next()
            DMA("pool", buf[:, 0:n], wb_d[l, :, o:o + n], [], [key])
            return buf, key

        def c_(name):
            o, n = coff[name]
            return cst_d[:, o:o + n]

        DMA("sp", cstf[:, 0:128], c_("ones"), [], [("cstf",)])
        DMA("sp", cstf[:, 128:256], c_("tri"), [], [("cstf",)])
        DMA("sp", cstf[:, 256:384], c_("utri"), [], [("cstf",)])
        DMA("sp", cstf[:, 384:512], c_("maskb"), [], [("cstf",)])
        DMA("sp", cstf[:, 512:640], c_("ident"), [], [("cstf",)])
        DMA("pool", cstb[:, 0:128], c_("ident"), [], [("cstb",)])
        DMA("pool", cstb[:, 128:256], c_("ones"), [], [("cstb",)])
        DMA("pool", cstb[:, 256:384], c_("blk64"), [], [("cstb",)])
        DMA("pool", cstb[:, 384:512], c_("tri"), [], [("cstb",)])
        DMA("pool", cstb[:, 512:640], c_("utri"), [], [("cstb",)])
        for c in range(8):
            DMA("sp", x[:, c, :], xT_d[c * 128:(c + 1) * 128, :], [], [("x",)])

        with ExitStack() as s0:
            oh = T("oh", [128, 384], F32, s0)
            invf = T("invf", [128, 2], F32, s0)
            DMA("sp", oh[:], c_("oh"), [], [("oh",)])
            DMA("sp", invf[:], c_("invf"), [], [("invf",)])
            pos_i = T("pos_i", [128, S], I32, s0)
            ang = T("ang", [128, S], F32, s0)
            tq = T("tq", [128, S], F32, s0)
            tk = T("tk", [128, S], F32, s0)
            ti = T("ti", [128, S], I32, s0)
            R_ = slice(64, 96)
            DMA("sp", pos_i[R_, :], pos_d, [], [("pos_i",)])
            CP(ang[R_, :], pos_i[R_, :], [("pos_i",)], [("ang",)])
            TS(ang[R_, :], ang[R_, :], invf[R_, 0:1], None, ALU.mult, None, [("ang",), ("invf",)], [("ang",)])
            for which, shift, dst in ((0, 0.0, sinS), (1, math.pi / 2, cosT)):
                TS(tq[R_, :], ang[R_, :], shift, None, ALU.add, None, [("ang",)], [("tq",)])
                TS(tk[R_, :], tq[R_, :], 1.0 / (2 * math.pi), None, ALU.mult, None, [("tq",)], [("tk",)])
                CP(ti[R_, :], tk[R_, :], [("tk",)], [("ti",)])
                CP(tk[R_, :], ti[R_, :], [("ti",)], [("tk",)])
                STT(tq[R_, :], tk[R_, :], -2 * math.pi, tq[R_, :], ALU.mult, ALU.add, [("tk",), ("tq",)], [("tq",)])
                TS(tk[R_, :], tq[R_, :], math.pi, 2 * math.pi, ALU.is_gt, ALU.mult, [("tq",)], [("tk",)])
                TT(tq[R_, :], tq[R_, :], tk[R_, :], ALU.subtract, [("tq",), ("tk",)], [("tq",)])
                TS(tk[R_, :], tq[R_, :], -math.pi, 2 * math.pi, ALU.is_lt, ALU.mult, [("tq",)], [("tk",)])
                TT(tq[R_, :], tq[R_, :], tk[R_, :], ALU.add, [("tq",), ("tk",)], [("tq",)])
                TS(tq[R_, :], tq[R_, :], math.pi, -math.pi, ALU.min, ALU.max, [("tq",)], [("tq",)])
                ACT(tk[R_, :], tq[R_, :], AF.Sin, [("tq",)], [("tk",)])
                if which == 0:
                    TS(dst[R_, :], tk[R_, :], invf[R_, 1:2], None, ALU.mult, None, [("tk",), ("invf",)], [("sinS",)])
                else:
                    CP(dst[R_, :], tk[R_, :], [("tk",)], [("cosT",)])
            relb = T("relb", [128, 4], F32, s0)
            tbh = T("tbh", [128, 128], F32, s0)
            gb = T("gb", [128, 384], F32, s0)
            DMA("sp", relb[0:32, :], relb_d, [], [("relb",)])
            P.add("dve", lambda e: e.memset(relb[32:33, :], NEG), [], [("relb",)])
            for h in range(4):
                TS(tbh[0:33, :], ones_f[0:33, :], relb[0:33, h:h + 1], None, ALU.mult, None, [("cstf",), ("relb",)], [("tbh",)])
                MM(ps[h % 2][:, 0:384], tbh[0:33, :], oh[0:33, :], True, True, [("tbh",), ("oh",)], [("ps", h % 2)])
                CP(gb[:], ps[h % 2][:, 0:384], [("ps", h % 2)], [("gb",)])
                CP(cbias[:, h:h + 1], gb[:, 382:383], [("gb",)], [("cbias",)])
                DMA("sp", zscr[h], gb[:], [("gb",)], [("zscr", h)])
                DMA("sp", band[:, h, :], bass.AP(zscr.tensor, h * 128 * 384 + 127, [[383, 128], [1, 256]]), [("zscr", h)], [("band",)])
            cv = T("cv", [128, 8], F32, s0)
            cact = T("cact", [128, 8], BF16, s0)
            DMA("sp", cv[:], cv_d, [], [("cv",)])
            ACT(cact[:], cv[:], AF.Silu, [("cv",)], [("cact",)])
            ada_rot = Rot("adaw", [hT[:, c_, hf_ * 1024:(hf_ + 1) * 1024] for c_ in range(8) for hf_ in range(2)])
            for l in range(nlayers):
                DMA("sp", sm[:], sm_d[l], [], [("sm",)])
                b = 2 + (l % 2)
                for j in range(48):
                    wt, wk = wload(l, f"ada{j}", ada_rot)
                    for k in range(8):
                        MM(ps[b][:, j:j + 1], wt[:, k * 128:(k + 1) * 128], cact[:, k:k + 1], k == 0, k == 7, [wk, ("cact",)], [("ps", b)])
                TT(modt[:, l * 48:(l + 1) * 48], ps[b][:, 0:48], smc("ada_b"), ALU.add, [("ps", b), ("sm",)], [("modt",)])
        P.barrier()

        prot = Rot("ps", [0, 1, 2])

        class Rot6:
            banks = [0, 1, 2, 4, 5, 6]

            def __init__(self):
                self.i = 0

            def next(self):
                b = self.banks[self.i % 6]
                self.i += 1
                return b, ("ps", b)

        prot6 = Rot6()

        def rstd_chain(out_sb, ss_ps, inv, r, w):
            ACT(out_sb, ss_ps, AF.Ln, r, w, bias=EPS, scale=inv)
            ACT(out_sb, out_sb, AF.Exp, w, w, scale=-0.5)

        def norm_to_hT(Gc, SHc, stk):
            xsq = T("xsq", [128, 8, 512], BF16, stk)
            rs = T("nrs", [128, 512], F32, stk)
            ntmp = [T(f"ntmp{i}", [128, 512], F32, stk) for i in range(2)]
            for tc in range(NTC):
                ts_ = slice(tc * 512, (tc + 1) * 512)
                for c in range(8):
                    if c % 2 == 0:
                        ACT(xsq[:, c, :], x[:, c, ts_], AF.Square, [("x", tc)], [("xsq", c)])
                    else:
                        TT(xsq[:, c, :], x[:, c, ts_], x[:, c, ts_], ALU.mult, [("x", tc)], [("xsq", c)])
                for c in range(8):
                    MM(ps[3][:], ones_b, xsq[:, c, :], c == 0, c == 7, [("xsq", c), ("cstb",)], [("ps", 3)])
                rstd_chain(rs[:], ps[3][:], 1.0 / D, [("ps", 3)], [("nrs",)])
                for c in range(8):
                    TT(ntmp[c % 2][:], x[:, c, ts_], rs[:], ALU.mult, [("x", tc), ("nrs",)], [("ntmp", c % 2)])
                    ACT(hT[:, c, ts_], ntmp[c % 2][:], AF.Identity, [("ntmp", c % 2), ("lsc",)], [("hT", tc)],
                        bias=SHc[:, c:c + 1], scale=Gc[:, c:c + 1], ss=("hT",))

        def conv_tc(ps_ap, pre, acc, nt, wcols, bcol, first, pk, ak, rps, halo=None, hk=None):
            H = nt - 1
            if first:
                P.add("dve", lambda e: e.memset(pre[:, 0:H], 0.0), [], [pk])
            elif halo is not None:
                CP(pre[:, 0:H], halo, [hk], [pk])
            ACT(pre[:, H:H + 512], ps_ap, AF.Identity, rps, [pk])
            ACT(acc, ps_ap, AF.Identity, rps + [("sm",)], [ak], bias=bcol, scale=wcols[nt - 1])
            for k in range(nt - 2, -1, -1):
                STT(acc, pre[:, k:k + 512], wcols[k], acc, ALU.mult, ALU.add, [pk, ak, ("sm",)], [ak])
            if halo is not None:
                CP(halo, pre[:, 512:512 + H], [pk], [hk])
            else:
                CP(pre[:, 0:H], pre[:, 512:512 + H], [pk], [pk])

        def conv_A(ps_ap, pre, acc, nt, wcols, bcol, pk, ak, rps):
            H = nt - 1
            ACT(pre[:, H:H + 512], ps_ap, AF.Identity, rps, [pk])
            ACT(acc, ps_ap, AF.Identity, rps + [("sm",)], [ak], bias=bcol, scale=wcols[nt - 1])

        def conv_B(pre, acc, nt, wcols, first, pk, ak, halo, hk):
            H = nt - 1
            if first:
                P.add("dve", lambda e: e.memset(pre[:, 0:H], 0.0), [], [pk])
            else:
                CP(pre[:, 0:H], halo, [hk], [pk])
            for k in range(nt - 2, -1, -1):
                STT(acc, pre[:, k:k + 512], wcols[k], acc, ALU.mult, ALU.add, [pk, ak, ("sm",)], [ak])
            CP(halo, pre[:, 512:512 + H], [pk], [hk])

        def attn_chunk(c, KT, QT, prow, kq_keys, V_get, vkey, bandap, BW, cb_, scale, bo, bs, Erot, tmpb, hook=None):
            LA = 2
            q0 = c * 512
            nk = 4 * c + 4
            pend = []
            for j in range(nk + LA):
                if j < nk:
                    k0 = j * 128
                    qs = max(q0, k0)
                    n = q0 + 512 - qs
                    col0 = qs - q0
                    b, bk = prot.next()
                    MM(ps[b][:, 0:n], KT[prow, k0:k0 + 128], QT[prow, qs:qs + n], True, True, kq_keys, [bk])
                    (E, ek) = Erot.next()
                    u0 = qs - k0
                    nband = max(0, min(n, BW - u0))
                    if nband > 0:
                        tb_, tbk = tmpb.next()
                        STT(tb_[:, 0:nband], ps[b][:, 0:nband], scale, bandap[:, u0:u0 + nband], ALU.mult, ALU.add,
                            [bk, ("band",), ("cstf",)], [tbk])
                        ACT(E[:, 0:nband], tb_[:, 0:nband], AF.Exp, [tbk], [ek])
                    if n > nband:
                        ACT(E[:, nband:n], ps[b][:, nband:n], AF.Exp, [bk, ("cbias",)], [ek], bias=cb_, scale=scale, ss=("E",))
                    pend.append((j, E, ek, n, col0))
                    if hook is not None:
                        hooks = hook if isinstance(hook, (list, tuple)) else [(2, hook)]
                        for (st_, fn_) in hooks:
                            if j == min(st_, nk - 1):
                                fn_()
                if j >= LA:
                    jj, E, ek, n, col0 = pend.pop(0)
                    MM(ps[bo][:, col0:col0 + n], V_get(jj), E[:, 0:n], jj == 0, jj == nk - 1, [vkey, ek], [("ps", bo)])
                    MM(ps[bs][:, col0:col0 + n], ones_b, E[:, 0:n], jj == 0, jj == nk - 1, [ek, ("cstb",)], [("ps", bs)])

        def merge_branch(l, yT, ykey, KX, wname, gbase, gm):
            with ExitStack() as sk:
                mg = T("mg", [128, 8, 1024], BF16, sk)
                sg = [T(f"sg{i}", [128, 512], F32, sk) for i in range(2)]
                wmr = Rot("wm", [T(f"wm{i}", [128, 1024], BF16, sk) for i in range(6)])
                yh = T("yh", [128, 8, 1024], BF16, sk) if yT is None else None
                for half in range(2):
                    if yT is None:
                        for k in range(8):
                            DMA("sp", yh[:, k, :], yscr[k, :, half * 1024:(half + 1) * 1024], [("yscr",)], [("yh",)])
                    for i in range(8):
                        wb, wbk = wload(l, f"{wname}{i}", wmr)
                        wg, wgk = wload(l, f"g{gbase + i}", wmr)
                        for t2 in range(2):
                            tc = half * 2 + t2
                            ts_ = slice(tc * 512, (tc + 1) * 512)
                            b1, k1 = prot6.next()
                            for k in range(KX):
                                rhs_ = yh[:, k, t2 * 512:(t2 + 1) * 512] if yT is None else yT[:, k, ts_]
                                MM(ps[b1][:], wb[:, k * 128:(k + 1) * 128], rhs_, k == 0, k == KX - 1, [wbk, ("yh",) if yT is None else ykey], [k1])
                            b2, k2 = prot6.next()
                            for k in range(8):
                                MM(ps[b2][:], wg[:, k * 128:(k + 1) * 128], hT[:, k, ts_], k == 0, k == 7, [wgk, ("hT", tc)], [k2])
                            ACT(sg[t2][:], ps[b2][:], AF.Sigmoid, [k2], [("sg", t2)])
                            TT(mg[:, i, t2 * 512:(t2 + 1) * 512], ps[b1][:], sg[t2][:], ALU.mult, [k1, ("sg", t2)], [("mg", t2)], ss=("mg",))
                    for io in range(8):
                        wo, wok = wload(l, f"wo{io}", wmr)
                        for t2 in range(2):
                            tc = half * 2 + t2
                            ts_ = slice(tc * 512, (tc + 1) * 512)
                            b, bk = prot6.next()
                            for k in range(8):
                                MM(ps[b][:], wo[:, k * 128:(k + 1) * 128], mg[:, k, t2 * 512:(t2 + 1) * 512], k == 0, k == 7, [wok, ("mg", t2)], [bk])
                            STT(x[:, io, ts_], ps[b][:], gm[:, io:io + 1], x[:, io, ts_], ALU.mult, ALU.add, [bk, ("x", tc), ("modt",)], [("x", tc)], ss=("x",))
            P.barrier()


        def mixer(l, g_m, neglam, sublnw):
            X_ = mybir.AxisListType.X
            if "ssd" in branches:
                with ExitStack() as sk:
                    ysT = T("ysT", [128, 8, S], BF16, sk)
                    with ExitStack() as s2:
                        pst = s2.enter_context(nc.psum_tensor(f"pst{l}", [128, 1024], BF16))
                        dtt, da, negcs, cd, dtend = [T(n_, [128, 256], F32, s2) for n_ in ("dtt", "da", "negcs", "cd", "dtend")]
                        scr = T("scr", [128, 1032], F32, s2)
                        cscol, csl, at_, e1 = scr[:, 0:256], scr[:, 256:512], scr[:, 512:768], scr[:, 768:1024]
                        halo4 = T("halo4", [128, 4, 4], F32, s2)
                        da_hi = T("da_hi", [128, 256], BF16, s2)
                        da_lo = T("da_lo", [128, 256], BF16, s2)
                        wdt, wdk = wload(l, "dt")
                        for tt in range(16):
                            for k in range(8):
                                MM(ps[4][:, tt * 16:(tt + 1) * 16], hT[:, k, tt * 128:(tt + 1) * 128], wdt[:, k * 16:(k + 1) * 16], k == 0, k == 7,
                                   [wdk, ("hT", tt // 4)], [("ps", 4)])
                        TT(e1[:], ps[4][:, 0:256], smc("dtb"), ALU.add, [("ps", 4), ("sm",)], [("e1",)])
                        ACT(e1[:], e1[:], AF.Exp, [("e1",)], [("e1",)])
                        ACT(dtt[:], e1[:], AF.Ln, [("e1",)], [("dtt",)], bias=1.0, scale=1.0)
                        ACT(at_[:], smc("alog"), AF.Exp, [("sm",)], [("at",)])
                        STT(da[:], at_[:], -1.0, dtt[:], ALU.mult, ALU.mult, [("at",), ("dtt",)], [("da",)])
                        ACT(da_hi[:], da[:], AF.Identity, [("da",)], [("da_hi",)])
                        ACT(e1[:], da_hi[:], AF.Identity, [("da_hi",)], [("e1",)])
                        TT(da_lo[:], da[:], e1[:], ALU.subtract, [("da",), ("e1",)], [("da_lo",)])
                        for tt in range(16):
                            MM(ps[5][:, tt * 16:(tt + 1) * 16], tri_f, da[:, tt * 16:(tt + 1) * 16], True, True, [("cstf",), ("da",)], [("ps", 5)])
                            MM(ps[6][:, tt * 16:(tt + 1) * 16], ones_f, da[:, tt * 16:(tt + 1) * 16], True, True, [("cstf",), ("da",)], [("ps", 6)])
                        CP(cscol[:], ps[5][:, 0:256], [("ps", 5)], [("cscol",)])
                        TS(negcs[:], ps[5][:, 0:256], -1.0, None, ALU.mult, None, [("ps", 5)], [("negcs",)])
                        CP(csl[:], ps[6][:, 0:256], [("ps", 6)], [("csl",)])
                        ACT(cd[:], ps[6][:, 0:256], AF.Exp, [("ps", 6)], [("cd",)])
                        TT(e1[:], csl[:], cscol[:], ALU.subtract, [("csl",), ("cscol",)], [("e1",)])
                        ACT(e1[:], e1[:], AF.Exp, [("e1",)], [("e1",)])
                        TT(dtend[:], e1[:], dtt[:], ALU.mult, [("e1",), ("dtt",)], [("dtend",)])
                        scw, scb, dcol, ssn = smc("scw"), smc("scb"), smc("ssd_d"), smc("ssn")
                        P.barrier()
                        if dbg.get("ssd_stop") == 1:
                            P.muted = True
                        for g in dbg.get("ssd_groups", range(4)):
                            with ExitStack() as s3:
                                xs_fm = T("xs_fm", [128, 2, S], BF16, s3)
                                bm_fm = T("bm_fm", [128, S], BF16, s3)
                                cm_fm = T("cm_fm", [128, S], BF16, s3)
                                pre = T("spre", [128, 515], F32, s3)
                                acc = T("sacc", [128, 512], F32, s3)
                                streams = ((2 * g, xs_fm[:, 0, :], ("xsfm", 0)), (2 * g + 1, xs_fm[:, 1, :], ("xsfm", 1)),
                                           (8 + g, bm_fm[:, :], ("bmfm",)), (12 + g, cm_fm[:, :], ("cmfm",)))
                                cbufs = ((pre, acc[:], ("spre", 0), ("sacc", 0)), (scr[:, 0:515], scr[:, 516:1028], ("spre", 1), ("sacc", 1)))
                                def sconv2(ctx):
                                    j, dst, dk, tc, si, sidx = ctx
                                    pre_, acc_, pk_, ak_ = cbufs[si]
                                    conv_B(pre_, acc_, 4, [scw[:, j * 4 + k:j * 4 + k + 1] for k in range(4)], tc == 0, pk_, ak_,
                                           halo4[:, sidx, 0:3], ("halo4", sidx))
                                    ACT(dst[:, tc * 512:(tc + 1) * 512], acc_, AF.Silu, [ak_], [dk], ss=(dk[0],))

                                for pair in (streams[0:2], streams[2:4]):
                                    wts = [wload(l, f"xbc{j}") for (j, _, _) in pair]
                                    prev_ = None
                                    for tc in range(NTC):
                                        ts_ = slice(tc * 512, (tc + 1) * 512)
                                        for si, (j, dst, dk) in enumerate(pair):
                                            wt, wk = wts[si]
                                            pre_, acc_, pk_, ak_ = cbufs[si]
                                            sidx = [q[0] for q in streams].index(j)
                                            b, bk = prot.next()
                                            for k in range(8):
                                                MM(ps[b][:], wt[:, k * 128:(k + 1) * 128], hT[:, k, ts_], k == 0, k == 7, [wk, ("hT", tc)], [bk])
                                            conv_A(ps[b][:], pre_, acc_, 4, [scw[:, j * 4 + k:j * 4 + k + 1] for k in range(4)], scb[:, j:j + 1], pk_, ak_, [bk])
                                            if prev_ is not None:
                                                sconv2(prev_)
                                            prev_ = (j, dst, dk, tc, si, sidx)
                                    sconv2(prev_)
                                Hs = T("Hs", [128, 256], F32, s3)
                                prevb = T("prevb", [128, 256], BF16, s3)
                                ysb = T("ysb", [128, 2, 512], F32, s3)
                                gz, grs = acc, pre

                                def two(name, shape, dt_, n=2):
                                    ts_l = [T(f"{name}{i}", shape, dt_, s3) for i in range(n)]
                                    return [ts_l[i % n] for i in range(2)]

                                xdt2, xdte2, bmt2 = two("xdt", [128, 256], BF16), two("xdte", [128, 256], BF16, 1), two("bmt", [128, 128], BF16, 1)
                                cbm2, dec2 = two("cbm", [128, 128], F32, 1), two("dec", [128, 512], F32, 1)
                                GT2, ecs2, Cs2 = two("GT", [128, 512], BF16), two("ecs", [128, 512], F32, 1), two("Cs", [128, 512], BF16)
                                cmf2, xsf2 = two("cmf", [128, 128], F32, 1), two("xsf", [128, 2, 128], F32)
                                Rh2 = two("Rh", [128, 512], BF16)
                                Rl2 = two("Rl", [128, 512], BF16)

                                def build_R(c, g=g):
                                    p = c % 2
                                    col = c * 16 + 4 * g
                                    v4 = lambda ap_: ap_.rearrange("p (h l) -> p h l", h=4)
                                    tb4 = tri_b.unsqueeze(1).broadcast_to([128, 4, 128])
                                    TT(v4(Rh2[p][:]), tb4, da_hi[:, col:col + 4].unsqueeze(2).broadcast_to([128, 4, 128]), ALU.mult, [("cstb",), ("da_hi",)], [("Rh", p)], eng="pool")
                                    TT(v4(Rl2[p][:]), tb4, da_lo[:, col:col + 4].unsqueeze(2).broadcast_to([128, 4, 128]), ALU.mult, [("cstb",), ("da_lo",)], [("Rl", p)], eng="pool")
                                P.add("dve", lambda e, Hs=Hs: e.memset(Hs[:], 0.0), [], [("Hs",)])

                                def stageA(c, g=g, xs_fm=xs_fm, bm_fm=bm_fm, cm_fm=cm_fm):
                                    p = c % 2
                                    tsl = slice(c * 128, (c + 1) * 128)
                                    col = c * 16 + 4 * g
                                    xdt, xdte, bmt, cbm, dec = xdt2[p], xdte2[p], bmt2[p], cbm2[p], dec2[p]
                                    GT, ecs, Cs, cmf, xsf = GT2[p], ecs2[p], Cs2[p], cmf2[p], xsf2[p]
                                    v4 = lambda ap_: ap_.rearrange("p (h l) -> p h l", h=4)
                                    Rh, Rl = Rh2[p], Rl2[p]
                                    if c == 0:
                                        build_R(0)
                                    for jj in range(2):
                                        P.add("pe", lambda e, o_=pst[:, jj * 128:(jj + 1) * 128], i_=xs_fm[:, jj, tsl]: e.transpose(o_, i_, ident_b),
                                              [("xsfm", jj), ("cstb",)], [("pst",)])
                                    P.add("pe", lambda e, o_=pst[:, 256:384], i_=bm_fm[:, tsl]: e.transpose(o_, i_, ident_b), [("bmfm",), ("cstb",)], [("pst",)])
                                    MM(ps[4 + p][:, 256:384], bm_fm[:, tsl], cm_fm[:, tsl], True, True, [("bmfm",), ("cmfm",)], [("ps", 4 + p)])
                                    MM(ps[6][:], utri_b, Rh[:], True, False, [("cstb",), ("Rh", p)], [("ps", 6)])
                                    MM(ps[6][:], utri_b, Rl[:], False, True, [("cstb",), ("Rl", p)], [("ps", 6)])
                                    MM(ps[3][:], ones_b, Rh[:], True, False, [("cstb",), ("Rh", p)], [("ps", 3)])
                                    MM(ps[3][:], ones_b, Rl[:], False, True, [("cstb",), ("Rl", p)], [("ps", 3)])
                                    ACT(cmf[:], cm_fm[:, tsl], AF.Identity, [("cmfm",)], [("cmf", 0)])
                                    for jj in range(2):
                                        ACT(xsf[:, jj, :], xs_fm[:, jj, tsl], AF.Identity, [("xsfm", jj)], [("xsf", p)])
                                    for h in range(4):
                                        hs = slice(h * 64, (h + 1) * 64)
                                        ACT(xdte[:, hs], pst[:, hs], AF.Identity, [("pst",), ("dtend",)], [("xdte", h)], scale=dtend[:, col + h:col + h + 1])
                                    ACT(bmt[:], pst[:, 256:384], AF.Identity, [("pst",)], [("bmt", 0)])
                                    for h in range(4):
                                        hs = slice(h * 64, (h + 1) * 64)
                                        ACT(xdt[:, hs], pst[:, hs], AF.Identity, [("pst",), ("dtt",)], [("xdt", p * 4 + h)], scale=dtt[:, col + h:col + h + 1])
                                    MM(ps[4 + p][:, 0:256], bmt[:], xdte[:], True, True, [("bmt", 0), ("xdte",)], [("ps", 4 + p)])
                                    TT(cbm[:], ps[4 + p][:, 256:384], tri_f, ALU.mult, [("ps", 4 + p), ("cstf",)], [("cbm", 0)])
                                    ACT(dec[:], ps[6][:], AF.Exp, [("ps", 6)], [("dec",)])
                                    ACT(ecs[:], ps[3][:], AF.Exp, [("ps", 3)], [("ecs", 0)])
                                    TT(v4(GT[:]), v4(dec[:]), cbm[:].unsqueeze(1).broadcast_to([128, 4, 128]), ALU.mult, [("dec",), ("cbm", 0)], [("GT", p)])
                                    TT(v4(Cs[:]), v4(ecs[:]), cmf[:].unsqueeze(1).broadcast_to([128, 4, 128]), ALU.mult, [("ecs", 0), ("cmf", 0)], [("Cs", p)])
                                    if c + 1 < 16:
                                        build_R(c + 1)

                                def stageB(c, g=g):
                                    p = c % 2
                                    col = c * 16 + 4 * g
                                    xdt, GT, Cs, xsf = xdt2[p], GT2[p], Cs2[p], xsf2[p]
                                    ACT(prevb[:], Hs[:], AF.Identity, [("Hs",)], [("prevb",)])
                                    v64 = lambda ap_: ap_.rearrange("p (h q) -> p h q", h=4)
                                    TT(v64(Hs[:]), v64(Hs[:]), cd[:, col:col + 4].unsqueeze(2).broadcast_to([128, 4, 64]), ALU.mult, [("Hs",), ("cd",)], [("Hs",)])
                                    TT(Hs[:], Hs[:], ps[4 + p][:, 0:256], ALU.add, [("Hs",), ("ps", 4 + p)], [("Hs",)])
                                    yb, ybk = prot.next()
                                    for h in range(4):
                                        hc = slice(h * 128, (h + 1) * 128)
                                        pc = slice((h // 2) * 128, (h // 2 + 1) * 128)
                                        MM(ps[yb][:, hc], xdt[:, pc], GT[:, hc], True, False, [("xdt", p * 4 + 2 * (h // 2)), ("xdt", p * 4 + 2 * (h // 2) + 1), ("GT", p)], [ybk])
                                        MM(ps[yb][:, hc], prevb[:, pc], Cs[:, hc], False, True, [("prevb",), ("Cs", p)], [ybk])
                                    c4 = c % 4
                                    for h in range(4):
                                        r0, p2 = (h % 2) * 64, h // 2
                                        rr = slice(r0, r0 + 64)
                                        STT(ysb[rr, p2, c4 * 128:(c4 + 1) * 128], xsf[rr, p2, :], dcol[rr, 2 * g + p2:2 * g + p2 + 1],
                                            ps[yb][rr, h * 128:(h + 1) * 128], ALU.mult, ALU.add, [("xsf", p), ("sm",), ybk], [("ysb", h)])
                                    if c4 == 3:
                                        tc = c // 4
                                        ts_ = slice(tc * 512, (tc + 1) * 512)
                                        for jj in range(2):
                                            wz, wzk = wload(l, f"z{2 * g + jj}")
                                            b, bk = prot.next()
                                            for k in range(8):
                                                MM(ps[b][:], wz[:, k * 128:(k + 1) * 128], hT[:, k, ts_], k == 0, k == 7, [wzk, ("hT", tc)], [bk])
                                            ACT(gz[:], ps[b][:], AF.Silu, [bk], [("sacc",)])
                                            TT(ysb[:, jj, :], ysb[:, jj, :], gz[:], ALU.mult, [("ysb",), ("sacc",)], [("ysb",)])
                                            ACT((GT2[1], Cs2[1])[jj][:], ysb[:, jj, :], AF.Square, [("ysb",)], [(("GT", 1), ("Cs", 1))[jj]])
                                        for jj in range(2):
                                            MM(ps[3][:], ones_b, (GT2[1], Cs2[1])[jj][:], jj == 0, jj == 1, [(("GT", 1), ("Cs", 1))[jj], ("cstb",)], [("ps", 3)])
                                        rstd_chain(grs[:, 0:512], ps[3][:], 1.0 / 256, [("ps", 3)], [("spre",)])
                                        for jj in range(2):
                                            STT(ysT[:, 2 * g + jj, ts_], ysb[:, jj, :], ssn[:, 2 * g + jj:2 * g + jj + 1], grs[:, 0:512], ALU.mult, ALU.mult,
                                                [("ysb",), ("spre",), ("sm",)], [("ysT",)], ss=("ysT",))

                                NCH = 16
                                stageA(0)
                                for c in range(NCH):
                                    if c + 1 < NCH:
                                        stageA(c + 1)
                                    stageB(c)
                            P.barrier()
                    P.barrier()
                    if l == 0:
                        tap("ysT", ysT[:, :, :], [128, 8, S], BF16, [("ysT",)])
                    merge_branch(l, ysT, ("ysT",), 8, "wc", 16, g_m)

            if "da" in branches:
                with ExitStack() as sk:
                    yaT = T("yaT", [128, 4, S], BF16, sk)
                    with ExitStack() as s2:
                        ps.append(s2.enter_context(nc.psum_tensor(f"ps7a{l}", [128, 512], F32)))
                        QN = T("QN", [128, S], BF16, s2)
                        wda = Rot("wda", [T(f"wda{i}", [128, 1024], BF16, s2) for i in range(4)])
                        KN = T("KN", [128, S], BF16, s2)
                        KN1 = T("KN1", [128, S], BF16, s2)
                        P.add("dve", lambda e, a_=KN[64:128, :]: e.memset(a_, 0.0), [], [("KN",)])
                        P.add("dve", lambda e, a_=KN1[0:64, :]: e.memset(a_, 0.0), [], [("KN1",)])
                        Vt = T("Vt", [128, S], BF16, s2)
                        Erot = Rot("E", [T(f"E{i}", [128, 512], BF16, s2) for i in range(3)])
                        tmpb = Rot("tmpb", [T(f"tmpb{i}", [128, 256], F32, s2) for i in range(2)])
                        rawR = Rot("raw", [T(f"raw{i}", [128, 512], F32, s2) for i in range(2)])
                        sqR = Rot("sq", [T(f"sq{i}", [128, 512], BF16, s2) for i in range(2)])
                        rsR = Rot("rs", [T(f"rs{i}", [128, 512], F32, s2) for i in range(2)])
                        ssR = Rot("ps", [3, 7])
                        on = [T(f"on{i}", [128, 512], F32, s2) for i in range(2)]
                        rinv = T("rinv", [128, 512], F32, s2)
                        oa = T("oa", [128, 512], F32, s2)
                        for h in range(4):
                            wq_ = {}

                            def qk_stage1(nm, dst, dk, wcol, tc, h=h, wq_=wq_):
                                if nm not in wq_:
                                    wq_[nm] = wload(l, f"{nm}{h}", wda)
                                wt, wk = wq_[nm]
                                ts_ = slice(tc * 512, (tc + 1) * 512)
                                b, bk = prot.next()
                                for k in range(8):
                                    MM(ps[b][:], wt[:, k * 128:(k + 1) * 128], hT[:, k, ts_], k == 0, k == 7, [wk, ("hT", tc)], [bk])
                                raw, rwk = rawR.next()
                                sq, sqk = sqR.next()
                                ACT(raw[:], ps[b][:], AF.Identity, [bk], [rwk])
                                ACT(sq[:], ps[b][:], AF.Square, [bk], [sqk])
                                return (raw, rwk, sq, sqk, dst, dk, wcol, ts_)

                            def qk_stage2(ctx):
                                raw, rwk, sq, sqk, dst, dk, wcol, ts_ = ctx
                                rs, rsk = rsR.next()
                                sb_, _ = ssR.next()
                                ssk = ("ps", sb_)
                                MM(ps[sb_][:], blk64_b, sq[:], True, True, [("cstb",), sqk], [ssk])
                                rstd_chain(rs[:], ps[sb_][:], 1.0 / 64, [ssk], [rsk])
                                if dk[0] == "KN":
                                    STT(KN[0:64, ts_], raw[0:64, :], wcol[0:64, 0:1], rs[0:64, :], ALU.mult, ALU.mult, [rwk, rsk, ("sm",)], [("KN",)], ss=("KN",))
                                    STT(KN1[64:128, ts_], raw[64:128, :], wcol[64:128, 0:1], rs[64:128, :], ALU.mult, ALU.mult, [rwk, rsk, ("sm",)], [("KN1",)], ss=("KN1",))
                                else:
                                    STT(dst[:, ts_], raw[:], wcol[:, 0:1], rs[:], ALU.mult, ALU.mult, [rwk, rsk, ("sm",)], [dk], ss=(dk[0],))

                            prev_ = None
                            for (nm, dst, dk, wcol) in (("qa", QN, ("QN",), smc("qn")), ("ka", KN, ("KN",), smc("kn"))):
                                for tc in range(NTC):
                                    cur_ = qk_stage1(nm, dst, dk, wcol, tc)
                                    if prev_ is not None:
                                        qk_stage2(prev_)
                                    prev_ = cur_
                            wv, wvk = wload(l, f"va{h}", wda)
                            for t4 in range(4):
                                b, bk = prot.next()
                                for i in range(4):
                                    tt = t4 * 4 + i
                                    for k in range(8):
                                        MM(ps[b][:, i * 128:(i + 1) * 128], hT[:, k, tt * 128:(tt + 1) * 128], wv[:, k * 128:(k + 1) * 128], k == 0, k == 7,
                                           [wvk, ("hT", t4)], [bk])
                                ACT(Vt[:, t4 * 512:(t4 + 1) * 512], ps[b][:], AF.Identity, [bk], [("Vt",)], ss=("Vt",))
                                if t4 == 0:
                                    qk_stage2(prev_)
                            dpend = [None]

                            dctx = {}

                            def da_finish(c, m, h=h):
                                bo, bs = ((4, 5), (6, 7))[m]
                                P.add("dve", lambda e, a_=rinv[:], b_=ps[bs][:]: e.reciprocal(a_, b_), [("ps", bs)], [("rinv",)])
                                TT(on[m][:], ps[bo][:], rinv[:], ALU.mult, [("ps", bo), ("rinv",)], [("on", m)])
                                if m == 1:
                                    STT(oa[:], on[1][:], neglam, on[0][:], ALU.mult, ALU.add, [("on", 0), ("on", 1), ("lsc",)], [("oa",)])
                                    sq, sqk = sqR.next()
                                    ACT(sq[:], oa[:], AF.Square, [("oa",)], [sqk])
                                    dctx["sq"] = (sq, sqk)

                            def da_finish2(c, m, h=h):
                                if m != 1:
                                    return
                                ts_ = slice(c * 512, (c + 1) * 512)
                                sq, sqk = dctx["sq"]
                                rs, rsk = rsR.next()
                                MM(ps[3][:], ones_b, sq[:], True, True, [("cstb",), sqk], [("ps", 3)])
                                rstd_chain(rs[:], ps[3][:], 1.0 / 128, [("ps", 3)], [rsk])
                                STT(yaT[:, h, ts_], oa[:], sublnw, rs[:], ALU.mult, ALU.mult, [("oa",), rsk, ("lsc",)], [("yaT",)], ss=("yaT",))

                            for c in range(NTC):
                                for m in range(2):
                                    bo, bs = ((4, 5), (6, 7))[m]
                                    attn_chunk(c, (KN, KN1)[m], QN, slice(0, 128), [(("KN",), ("KN1",))[m], ("QN",)], lambda j: Vt[:, j * 128:(j + 1) * 128], ("Vt",),
                                               band[:, h, :], 256, cbias[:, h:h + 1], 0.125, bo, bs, Erot, tmpb, hook=dpend[0])
                                    dpend[0] = [(2, (lambda c=c, m=m: da_finish(c, m))), (7, (lambda c=c, m=m: da_finish2(c, m)))]
                            for (_, fn_) in dpend[0]:
                                fn_()
                        P.barrier()
                        ps.pop()
                    if l == 0:
                        tap("yaT", yaT[:, :, :], [128, 4, S], BF16, [("yaT",)])
                    merge_branch(l, yaT, ("yaT",), 4, "wa", 0, g_m)

            if "mla" in branches:
                with ExitStack() as sk:
                    ybT = T("ybT", [128, 4, S], BF16, sk)
                    with ExitStack() as s2:
                        ps.append(s2.enter_context(nc.psum_tensor(f"ps7b{l}", [128, 512], F32)))
                        cqn = T("cqn", [128, 2, S], BF16, s2)
                        wml = Rot("wml", [T(f"wml{i}", [128, 1024], BF16, s2) for i in range(4)])
                        ckvn = T("ckvn", [128, S], BF16, s2)
                        krope = T("krope", [128, S], BF16, s2)
                        krsq = T("krsq", [128, S], BF16, s2)
                        Vb = T("Vb", [128, S], BF16, s2)
                        QH = T("QH", [128, S], BF16, s2)
                        KH = T("KH", [128, S], BF16, s2)
                        Erot = Rot("E", [T(f"Eb{i}", [128, 512], BF16, s2) for i in range(3)])
                        tmpb = Rot("tmpb", [T(f"tmpbm{i}", [128, 128], F32, s2) for i in range(2)])
                        sq2 = T("sq2", [128, 2, 512], BF16, s2)
                        sq4 = T("sq4", [128, 4, 512], BF16, s2)
                        rs = T("mrs", [128, 512], F32, s2)
                        rsB = T("mrsB", [128, 512], F32, s2)
                        t1 = T("t1", [128, 512], F32, s2)
                        t2 = T("t2", [128, 512], F32, s2)
                        cf = T("cf", [128, 512], F32, s2)
                        sf = T("sf", [128, 512], F32, s2)
                        raw2 = [t1, t2]
                        rinv = t2
                        qan, kvan, mqn, mkn = smc("qan"), smc("kvan"), smc("mqn"), smc("mkn")
                        R_ = slice(64, 96)
                        wcq = [wload(l, "cq0", wml), wload(l, "cq1", wml)]
                        for tc in range(NTC):
                            ts_ = slice(tc * 512, (tc + 1) * 512)
                            for j in range(2):
                                b, bk = prot.next()
                                for k in range(8):
                                    MM(ps[b][:], wcq[j][0][:, k * 128:(k + 1) * 128], hT[:, k, ts_], k == 0, k == 7, [wcq[j][1], ("hT", tc)], [bk])
                                ACT(raw2[j][:], ps[b][:], AF.Identity, [bk], [(("t1",), ("t2",))[j]])
                                ACT(sq2[:, j, :], ps[b][:], AF.Square, [bk], [("sq2", j)])
                            for j in range(2):
                                MM(ps[3][:], ones_b, sq2[:, j, :], j == 0, j == 1, [("cstb",), ("sq2", j)], [("ps", 3)])
                            rstd_chain(rs[:], ps[3][:], 1.0 / 256, [("ps", 3)], [("mrs",)])
                            for j in range(2):
                                STT(cqn[:, j, ts_], raw2[j][:], qan[:, j:j + 1], rs[:], ALU.mult, ALU.mult, [(("t1",), ("t2",))[j], ("mrs",), ("sm",)], [("cqn",)], ss=("cqn",))
                        wc_, wck = wload(l, "ckv", wml)
                        for tc in range(NTC):
                            ts_ = slice(tc * 512, (tc + 1) * 512)
                            b, bk = prot.next()
                            for k in range(8):
                                MM(ps[b][:], wc_[:, k * 128:(k + 1) * 128], hT[:, k, ts_], k == 0, k == 7, [wck, ("hT", tc)], [bk])
                            ACT(raw2[0][:], ps[b][:], AF.Identity, [bk], [("t1",)])
                            ACT(sq2[:, 0, :], ps[b][:], AF.Square, [bk], [("sq2", 0)])
                            MM(ps[3][:], ones_b, sq2[:, 0, :], True, True, [("cstb",), ("sq2", 0)], [("ps", 3)])
                            rstd_chain(rs[:], ps[3][:], 1.0 / 128, [("ps", 3)], [("mrs",)])
                            STT(ckvn[:, ts_], raw2[0][:], kvan[:, 0:1], rs[:], ALU.mult, ALU.mult, [("t1",), ("mrs",), ("sm",)], [("ckvn",)], ss=("ckvn",))
                        wkr, wkrk = wload(l, "kr", wml)
                        wkp, wkpk = wload(l, "krp", wml)
                        for tc in range(NTC):
                            ts_ = slice(tc * 512, (tc + 1) * 512)
                            b1, k1 = prot.next()
                            for k in range(8):
                                MM(ps[b1][0:96, :], wkr[:, k * 96:(k + 1) * 96], hT[:, k, ts_], k == 0, k == 7, [wkrk, ("hT", tc)], [k1])
                            b2, k2 = prot.next()
                            for k in range(8):
                                MM(ps[b2][0:96, :], wkp[:, k * 96:(k + 1) * 96], hT[:, k, ts_], k == 0, k == 7, [wkpk, ("hT", tc)], [k2])
                            ACT(krsq[R_, ts_], ps[b1][R_, :], AF.Square, [k1], [("krsq",)], ss=("krsq",))
                            ACT(cf[R_, :], cosT[R_, ts_], AF.Identity, [("cosT",)], [("cf",)])
                            ACT(sf[R_, :], sinS[R_, ts_], AF.Identity, [("sinS",)], [("sf",)])
                            STT(t1[R_, :], ps[b1][R_, :], mkn[R_, 0:1], cf[R_, :], ALU.mult, ALU.mult, [k1, ("sm",), ("cf",)], [("t1",)])
                            STT(t2[R_, :], ps[b2][R_, :], mkn[R_, 1:2], sf[R_, :], ALU.mult, ALU.mult, [k2, ("sm",), ("sf",)], [("t2",)])
                            TT(t1[R_, :], t1[R_, :], t2[R_, :], ALU.add, [("t1",), ("t2",)], [("t1",)])
                            ACT(krope[R_, ts_], t1[R_, :], AF.Identity, [("t1",)], [("krope",)], ss=("krope",))
                        for h in range(8):
                            if h % 2 == 0:
                                wv, wvk = wload(l, f"ukv{h // 2}", wml)
                                for t4 in range(4):
                                    b, bk = prot.next()
                                    for i in range(4):
                                        tt = t4 * 4 + i
                                        MM(ps[b][:, i * 128:(i + 1) * 128], ckvn[:, tt * 128:(tt + 1) * 128], wv[:, 0:128], True, True, [wvk, ("ckvn",)], [bk])
                                    ACT(Vb[:, t4 * 512:(t4 + 1) * 512], ps[b][:], AF.Identity, [bk], [("Vb",)], ss=("Vb",))
                            wq, wqk = wload(l, f"uq{h}", wml)
                            wqp, wqpk = wload(l, f"uqp{h}", wml)
                            wkn, wknk = wload(l, f"ukn{h}", wml)
                            mrot = Rot6()

                            def mla_stage1(tc, wq=wq, wqk=wqk, wqp=wqp, wqpk=wqpk, wkn=wkn, wknk=wknk):
                                ts_ = slice(tc * 512, (tc + 1) * 512)
                                par = tc % 2
                                b1, k1 = mrot.next()
                                for k in range(2):
                                    MM(ps[b1][0:96, :], wq[:, k * 96:(k + 1) * 96], cqn[:, k, ts_], k == 0, k == 1, [wqk, ("cqn",)], [k1])
                                b2, k2 = mrot.next()
                                for k in range(2):
                                    MM(ps[b2][0:96, :], wqp[:, k * 96:(k + 1) * 96], cqn[:, k, ts_], k == 0, k == 1, [wqpk, ("cqn",)], [k2])
                                b3, k3 = mrot.next()
                                MM(ps[b3][0:64, :], wkn[:, 0:64], ckvn[:, ts_], True, True, [wknk, ("ckvn",)], [k3])
                                ACT(sq4[0:96, par, :], ps[b1][0:96, :], AF.Square, [k1], [("sq4", par)])
                                ACT(sq4[0:64, 2 + par, :], ps[b3][0:64, :], AF.Square, [k3], [("sq4", 2 + par)])
                                return (ts_, par, b1, k1, b2, k2, b3, k3)

                            def mla_stage2(ctx):
                                ts_, par, b1, k1, b2, k2, b3, k3 = ctx
                                MM(ps[3][0:96, :], ones_b[0:96, 0:96], sq4[0:96, par, :], True, True, [("cstb",), ("sq4", par)], [("ps", 3)])
                                MM(ps[7][0:96, :], ones_b[0:64, 0:96], sq4[0:64, 2 + par, :], True, False, [("cstb",), ("sq4", 2 + par)], [("ps", 7)])
                                MM(ps[7][0:96, :], ones_b[R_, 0:96], krsq[R_, ts_], False, True, [("cstb",), ("krsq",)], [("ps", 7)])
                                rstd_chain(rs[0:96, :], ps[3][0:96, :], 1.0 / 96, [("ps", 3)], [("mrs",)])
                                rstd_chain(rsB[0:96, :], ps[7][0:96, :], 1.0 / 96, [("ps", 7)], [("mrsB",)])
                                STT(QH[0:64, ts_], ps[b1][0:64, :], mqn[0:64, 0:1], rs[0:64, :], ALU.mult, ALU.mult, [k1, ("mrs",), ("sm",)], [("QH",)], ss=("QH",))
                                ACT(cf[R_, :], cosT[R_, ts_], AF.Identity, [("cosT",)], [("cf",)])
                                ACT(sf[R_, :], sinS[R_, ts_], AF.Identity, [("sinS",)], [("sf",)])
                                STT(t1[R_, :], ps[b1][R_, :], mqn[R_, 0:1], cf[R_, :], ALU.mult, ALU.mult, [k1, ("sm",), ("cf",)], [("t1",)])
                                STT(t2[R_, :], ps[b2][R_, :], mqn[R_, 1:2], sf[R_, :], ALU.mult, ALU.mult, [k2, ("sm",), ("sf",)], [("t2",)])
                                TT(t1[R_, :], t1[R_, :], t2[R_, :], ALU.add, [("t1",), ("t2",)], [("t1",)])
                                TT(QH[R_, ts_], t1[R_, :], rs[R_, :], ALU.mult, [("t1",), ("mrs",)], [("QH",)], ss=("QH",))
                                STT(KH[0:64, ts_], ps[b3][0:64, :], mkn[0:64, 0:1], rsB[0:64, :], ALU.mult, ALU.mult, [k3, ("mrsB",), ("sm",)], [("KH",)], ss=("KH",))
                                ACT(sf[R_, :], krope[R_, ts_], AF.Identity, [("krope",)], [("sf",)])
                                TT(KH[R_, ts_], sf[R_, :], rsB[R_, :], ALU.mult, [("sf",), ("mrsB",)], [("KH",)], ss=("KH",))

                            prev_ = None
                            for tc in range(NTC):
                                cur_ = mla_stage1(tc)
                                if prev_ is not None:
                                    mla_stage2(prev_)
                                prev_ = cur_
                            mla_stage2(prev_)
                            mpend = [None]

                            def mla_finish(c, h=h):
                                ts_ = slice(c * 512, (c + 1) * 512)
                                bo, bs = ((4, 5), (6, 7))[c % 2]
                                rr = slice((h % 2) * 64, (h % 2) * 64 + 64)
                                P.add("dve", lambda e, a_=rinv[rr, :], b_=ps[bs][rr, :]: e.reciprocal(a_, b_), [("ps", bs)], [("t2",)])
                                TT(ybT[rr, h // 2, ts_], ps[bo][rr, :], rinv[rr, :], ALU.mult, [("ps", bo), ("t2",)], [("ybT",)], ss=("ybT",))

                            for c in range(NTC):
                                bo, bs = ((4, 5), (6, 7))[c % 2]
                                attn_chunk(c, KH, QH, slice(0, 96), [("KH",), ("QH",)], lambda j: Vb[:, j * 128:(j + 1) * 128], ("Vb",),
                                           maskb, 128, 0.0, 96 ** -0.5, bo, bs, Erot, tmpb, hook=mpend[0])
                                mpend[0] = (lambda c=c: mla_finish(c))
                            mpend[0]()
                        P.barrier()
                        ps.pop()
                    if l == 0:
                        tap("ybT", ybT[:, :, :], [128, 4, S], BF16, [("ybT",)])
                    merge_branch(l, ybT, ("ybT",), 4, "wb", 8, g_m)
            if l == 0:
                tap("x_mid", x[:, :, :], [128, 8, S], F32, [("x",)])

        for l in range(nlayers):
            lam_init = 0.8 - 0.6 * math.exp(-0.3 * l)
            DMA("sp", sm[:], sm_d[l], [], [("sm",)])
            mod = modt[:, l * 48:(l + 1) * 48]
            sh_m, sc_m, g_m = mod[:, 0:8], mod[:, 8:16], mod[:, 16:24]
            sh_f, sc_f, g_f = mod[:, 24:32], mod[:, 32:40], mod[:, 40:48]
            G1, G2 = lsc[:, 0:8], lsc[:, 8:16]
            neglam, sublnw = lsc[:, 16:17], lsc[:, 17:18]
            STT(G1, sc_m, 1.0, smc("norm_mix"), ALU.add, ALU.mult, [("modt",), ("sm",)], [("lsc",)])
            STT(G2, sc_f, 1.0, smc("norm_ffn"), ALU.add, ALU.mult, [("modt",), ("sm",)], [("lsc",)])
            lp = smc("lam")
            lt = lsc[:, 24:28]
            with ExitStack() as sk:
                lpp = T("lpp", [128, 128], F32, sk)
                TT(lpp[:, 0:64], lp[:, 0:64], lp[:, 64:128], ALU.mult, [("sm",)], [("lpp",)])
                TT(lpp[:, 64:128], lp[:, 128:192], lp[:, 192:256], ALU.mult, [("sm",)], [("lpp",)])
                P.add("dve", lambda e, lpp=lpp, lt=lt: e.reduce_sum(lt[:, 0:1], lpp[:, 0:64], mybir.AxisListType.X), [("lpp",)], [("lsc",)])
                P.add("dve", lambda e, lpp=lpp, lt=lt: e.reduce_sum(lt[:, 1:2], lpp[:, 64:128], mybir.AxisListType.X), [("lpp",)], [("lsc",)])
                ACT(lt[:, 0:2], lt[:, 0:2], AF.Exp, [("lsc",)], [("lsc",)])
                TT(lt[:, 2:3], lt[:, 1:2], lt[:, 0:1], ALU.subtract, [("lsc",)], [("lsc",)])
                TS(neglam, lt[:, 2:3], -lam_init, None, ALU.add, None, [("lsc",)], [("lsc",)])
                TS(sublnw, smc("subln"), 1.0 - lam_init, None, ALU.mult, None, [("sm",)], [("lsc",)])
                norm_to_hT(G1, sh_m, sk)
            P.barrier()
            if l == 0:
                tap("hT", hT[:, :, :], [128, 8, S], BF16, [("hT",)])

            if do_mixer:
                mixer(l, g_m, neglam, sublnw)
                if P.muted:
                    P.muted = False
                    P.barrier()

            if do_ffn:
                with ExitStack() as sk:
                    norm_to_hT(G2, sh_f, sk)
                P.barrier()
                if l == 0:
                    tap("h2T", hT[:, :, :], [128, 8, S], BF16, [("hT",)])
                with ExitStack() as sk:
                    gbuf = T("gbuf", [128, NF, 1024], BF16, sk)
                    pre = [[T(f"fpre{w_}{p_}", [128, 514], F32, sk) for p_ in range(2)] for w_ in range(2)]
                    acc = [[T(f"facc{w_}{p_}", [128, 512], F32, sk) for p_ in range(2)] for w_ in range(2)]
                    halo = T("halo", [128, 44, 2], F32, sk)
                    hl = T("hl", [128, 2, 2], F32, sk)
                    wbig = [T(f"wbig{i}", [128, NF * 128], BF16, sk) for i in range(2)]
                    bigrot = Rot("wbig", wbig)
                    wffn = Rot("wf", [T(f"wf{i}", [128, 1024], BF16, sk) for i in range(6)])
                    fcw, fcb = smc("fcw"), smc("fcb")
                    fit = [0]

                    def ffn_stage1(half, i, t2, wts):
                        tc = half * 2 + t2
                        ts_ = slice(tc * 512, (tc + 1) * 512)
                        par = fit[0] % 2
                        fit[0] += 1
                        for which in range(2):
                            wt, wk = wts[which]
                            j = which * NF + i
                            b, bk = prot6.next()
                            for k in range(8):
                                MM(ps[b][:], wt[:, k * 128:(k + 1) * 128], hT[:, k, ts_], k == 0, k == 7, [wk, ("hT", tc)], [bk])
                            conv_A(ps[b][:], pre[which][par], acc[which][par][:], 3, [fcw[:, j * 3 + k:j * 3 + k + 1] for k in range(3)],
                                   fcb[:, j:j + 1], ("fpre", which * 2 + par), ("facc", which * 2 + par), [bk])
                        return (half, i, t2, par)

                    def ffn_stage2(ctx):
                        half, i, t2, par = ctx
                        tc = half * 2 + t2
                        for which in range(2):
                            j = which * NF + i
                            pk, ak = ("fpre", which * 2 + par), ("facc", which * 2 + par)
                            hk = ("hl", which)
                            if tc == 2:
                                CP(hl[:, which, :], halo[:, j, :], [("halo", j)], [hk])
                            conv_B(pre[which][par], acc[which][par][:], 3, [fcw[:, j * 3 + k:j * 3 + k + 1] for k in range(3)], tc == 0, pk, ak,
                                   hl[:, which, :], hk)
                            if tc == 1:
                                CP(halo[:, j, :], hl[:, which, :], [hk], [("halo", j)])
                        a0, a1 = acc[0][par], acc[1][par]
                        ACT(a0[:], a0[:], AF.Silu, [("facc", par)], [("facc", par)])
                        TT(gbuf[:, i, t2 * 512:(t2 + 1) * 512], a0[:], a1[:], ALU.mult, [("facc", par), ("facc", 2 + par)], [("gbuf", i * 2 + t2)])

                    for half in range(2):
                        prev_ = None
                        for i in range(NF):
                            wts = [wload(l, f"ua{i}", wffn), wload(l, f"uv{i}", wffn)]
                            for t2 in range(2):
                                cur_ = ffn_stage1(half, i, t2, wts)
                                if prev_ is not None:
                                    ffn_stage2(prev_)
                                prev_ = cur_
                        ffn_stage2(prev_)
                        for io in range(8):
                            wd, wdk = wload(l, f"dn{io}", bigrot)
                            for t2 in range(2):
                                tc = half * 2 + t2
                                ts_ = slice(tc * 512, (tc + 1) * 512)
                                b, bk = prot6.next()
                                for k in range(NF):
                                    MM(ps[b][:], wd[:, k * 128:(k + 1) * 128], gbuf[:, k, t2 * 512:(t2 + 1) * 512], k == 0, k == NF - 1,
                                       [wdk, ("gbuf", k * 2 + t2)], [bk])
                                STT(x[:, io, ts_], ps[b][:], g_f[:, io:io + 1], x[:, io, ts_], ALU.mult, ALU.add, [bk, ("x", tc), ("modt",)], [("x", tc)], ss=("x",))
                P.barrier()
            if l == 0:
                tap("x_l0", x[:, :, :], [128, 8, S], F32, [("x",)])

        outs = []
        for c in range(8):
            outs.append(DMA("sp", out_d[c * 128:(c + 1) * 128, :], x[:, c, :], [("x",)], [("out", c)]))
        P.add("sp", None, [("out", c) for c in range(8)] + [("tapout", n) for n in tap_d], [])
        info = P.emit(nc, st)
    return nc, info, list(tap_d.keys())


def _host_inputs(inputs, cores):
    inp = {k: np.asarray(v) for k, v in inputs.items()}
    wblob = np.stack([_weight_blob(inp, l) for l in range(NL)])
    smb = np.stack([_small_blob(inp, l).build() for l in range(NL)])
    cst = _const_blob().build()
    relb = np.ascontiguousarray(inp["rel_bias"], dtype=np.float32)
    maps = []
    for b in cores:
        maps.append({
            "xT": np.ascontiguousarray(inp["x"][b].T, dtype=np.float32),
            "cst": cst, "sm": smb,
            "cvec": _col(inp["c"][b]),
            "pos": np.ascontiguousarray(np.broadcast_to(inp["positions"][b][None, :], (32, S)), dtype=np.int32),
            "relb": relb, "wblob": wblob,
        })
    return maps


def kernel(**inputs):
    nc, info, _ = build()
    maps = _host_inputs(inputs, range(8))
    res = run_bass_kernel_spmd(nc, maps, core_ids=list(range(8)))
    return np.stack([np.ascontiguousarray(res.results[b]["outT"].T) for b in range(8)]).astype(np.float32)
```
